# Optimizing a Trainium2 kernel written in Bass

```python
import jax, jax.numpy as jnp
from jax import lax
import numpy as np

D_MODEL = 1024
BATCH = 4
SEQ = 8192
DEPTH = 4

GRID_W = 64
CTX_LEN = 256
N_EVEN = (DEPTH + 1) // 2
N_ODD = DEPTH // 2
N_SUB = 3
N_MOD = 3 * N_SUB
D_FF = ((8 * D_MODEL // 3) + 255) // 256 * 256
FFN_RES = 0.5
LRU_WIDTH = D_MODEL // 2
LRU_BW = 64
LRU_BLOCKS = LRU_WIDTH // LRU_BW
LRU_C = 8.0
CONV_W = 4
WIN_HD = 64
WIN_HEADS = (D_MODEL // 2) // WIN_HD
WIN_KV = WIN_HEADS // 4
WIN_G = WIN_HEADS // WIN_KV
WINDOW = 128
GLB_HD = 128
GLB_HEADS = D_MODEL // GLB_HD
GLB_KV = GLB_HEADS // 4
GLB_G = GLB_HEADS // GLB_KV
Q_BLOCK = 128
ROPE_THETA = 10000.0
EPS = 1e-6
EVEN_SPLITS = (LRU_WIDTH, LRU_WIDTH, WIN_HEADS * WIN_HD, WIN_KV * WIN_HD, WIN_KV * WIN_HD)
ODD_SPLITS = (GLB_HEADS * GLB_HD, GLB_KV * GLB_HD, GLB_KV * GLB_HD)
EVEN_IN = sum(EVEN_SPLITS)
ODD_IN = sum(ODD_SPLITS)
EVEN_OUT = LRU_WIDTH + WIN_HEADS * WIN_HD
ODD_OUT = GLB_HEADS * GLB_HD

kernel_name = 'hybrid_dit_rglru_window_axial_block'


def _rmsnorm(x, g):
    xf = x.astype(jnp.float32)
    y = xf * lax.rsqrt(jnp.mean(xf * xf, axis=-1, keepdims=True) + EPS)
    return (y * g.astype(jnp.float32)).astype(x.dtype)


def _modulate(h, shift, scale):
    return h * (1 + scale) + shift


def _split(p, sizes):
    out, o = [], 0
    for s in sizes:
        out.append(p[..., o:o + s])
        o += s
    return out


def _axial_rope(row, col, hd):
    n_freq = hd // 4
    freq = ROPE_THETA ** (-jnp.arange(n_freq, dtype=jnp.float32) / n_freq)
    ang = jnp.concatenate([row.astype(jnp.float32)[:, None] * freq,
                           col.astype(jnp.float32)[:, None] * freq], axis=-1)
    return jnp.cos(ang), jnp.sin(ang)


def _apply_rope(x, rope):
    cos, sin = rope
    hd = x.shape[-1]
    xf = x.astype(jnp.float32).reshape(x.shape[:-1] + (hd // 2, 2))
    shape = (x.shape[1],) + (1,) * (x.ndim - 3) + (hd // 2,)
    cos = cos.reshape(shape)
    sin = sin.reshape(shape)
    x1, x2 = xf[..., 0], xf[..., 1]
    out = jnp.stack([x1 * cos - x2 * sin, x1 * sin + x2 * cos], axis=-1).reshape(x.shape)
    return out.astype(x.dtype)


def _gqa_attend(q, k, v, mask, sink):
    scale = q.shape[-1] ** -0.5
    s = jnp.einsum('bqkgd,bskd->bkgqs', q, k).astype(jnp.float32) * scale
    if mask is not None:
        s = jnp.where(mask, s, -jnp.inf)
    if sink is not None:
        snk = jnp.broadcast_to(sink.astype(jnp.float32)[None, :, :, None, None], s.shape[:-1] + (1,))
        p = jax.nn.softmax(jnp.concatenate([s, snk], axis=-1), axis=-1)[..., :-1]
    else:
        p = jax.nn.softmax(s, axis=-1)
    return jnp.einsum('bkgqs,bskd->bqkgd', p.astype(v.dtype), v)


def _window_attention(q, k, v, kc, vc, sink):
    B_, T = q.shape[0], q.shape[1]
    nb = T // Q_BLOCK
    band = Q_BLOCK + 2 * WINDOW
    pad = ((0, 0), (WINDOW, WINDOW), (0, 0), (0, 0))
    kp = jnp.pad(k, pad)
    vp = jnp.pad(v, pad)
    qb = q.reshape((B_, nb, Q_BLOCK) + q.shape[2:]).swapaxes(0, 1)
    ctx_valid = jnp.ones((Q_BLOCK, kc.shape[1]), dtype=bool)

    def block(args):
        n, q_blk = args
        start = n * Q_BLOCK
        k_loc = lax.dynamic_slice_in_dim(kp, start, band, axis=1)
        v_loc = lax.dynamic_slice_in_dim(vp, start, band, axis=1)
        qpos = start + jnp.arange(Q_BLOCK)
        kpos = start - WINDOW + jnp.arange(band)
        valid = (jnp.abs(kpos[None, :] - qpos[:, None]) <= WINDOW) & (kpos[None, :] >= 0) & (kpos[None, :] < T)
        mask = jnp.concatenate([valid, ctx_valid], axis=1)
        return _gqa_attend(q_blk, jnp.concatenate([k_loc, kc], axis=1),
                           jnp.concatenate([v_loc, vc], axis=1), mask, sink)

    out = lax.map(block, (jnp.arange(nb), qb))
    return out.swapaxes(0, 1).reshape(B_, T, -1)


def _dense_attention(q, k_all, v_all):
    B_, T = q.shape[0], q.shape[1]
    nb = T // Q_BLOCK
    qb = q.reshape((B_, nb, Q_BLOCK) + q.shape[2:]).swapaxes(0, 1)
    out = lax.map(lambda q_blk: _gqa_attend(q_blk, k_all, v_all, None, None), qb)
    return out.swapaxes(0, 1).reshape(B_, T, -1)


def _centred_dwconv(x, w, b):
    left = CONV_W // 2
    right = CONV_W - 1 - left
    L = x.shape[1]
    xp = jnp.pad(x, ((0, 0), (left, right), (0, 0)))
    out = b
    for t in range(CONV_W):
        out = out + xp[:, t:t + L] * w[t]
    return out


def _rglru_coeffs(xin, w_a, b_a, w_x, b_x, lam):
    B_, L, C = xin.shape
    xb = xin.reshape(B_, L, LRU_BLOCKS, LRU_BW)
    gate_a = jnp.einsum('blhi,hij->blhj', xb, w_a).reshape(B_, L, C) + b_a
    gate_x = jnp.einsum('blhi,hij->blhj', xb, w_x).reshape(B_, L, C) + b_x
    r = jax.nn.sigmoid(gate_a.astype(jnp.float32))
    i = jax.nn.sigmoid(gate_x.astype(jnp.float32))
    log_a = -LRU_C * r * jax.nn.softplus(-lam.astype(jnp.float32))
    a = jnp.exp(log_a)
    b = jnp.sqrt(-jnp.expm1(2.0 * log_a)) * i * xin.astype(jnp.float32)
    return a, b


def _combine(e1, e2):
    a1, b1 = e1
    a2, b2 = e2
    return a1 * a2, a2 * b1 + b2


def _linear_scan(a, b, h0, reverse):
    if h0 is not None:
        if reverse:
            b = b.at[:, -1].add(a[:, -1] * h0)
        else:
            b = b.at[:, 0].add(a[:, 0] * h0)
    _, h = lax.associative_scan(_combine, (a, b), reverse=reverse, axis=1)
    return h


def _ffn_sublayer(h, shift, scale, gate, g_pre, g_post, w_gate, w_up, w_down):
    u = _modulate(_rmsnorm(h, g_pre), shift, scale)
    y = (jax.nn.silu(u @ w_gate) * (u @ w_up)) @ w_down
    return h + FFN_RES * gate * _rmsnorm(y, g_post)


def _even_mixer(u, uc, w_in, conv_w, conv_b, w_a, b_a, w_x, b_x, lam, sink, w_out, rope, with_ctx_out):
    B_, T, _ = u.shape
    Lc = uc.shape[1]
    xa, ga, q, k, v = _split(u @ w_in, EVEN_SPLITS)
    xac, gac, qc, kc, vc = _split(uc @ w_in, EVEN_SPLITS)
    xa = _centred_dwconv(xa, conv_w, conv_b)
    xac = _centred_dwconv(xac, conv_w, conv_b)
    a_cf, b_cf = _rglru_coeffs(xac, w_a[0], b_a[0], w_x[0], b_x[0], lam[0])
    a_cb, b_cb = _rglru_coeffs(xac, w_a[1], b_a[1], w_x[1], b_x[1], lam[1])
    h_cf = _linear_scan(a_cf, b_cf, None, False)
    h_cb = _linear_scan(a_cb, b_cb, None, True)
    a_f, b_f = _rglru_coeffs(xa, w_a[0], b_a[0], w_x[0], b_x[0], lam[0])
    a_b, b_b = _rglru_coeffs(xa, w_a[1], b_a[1], w_x[1], b_x[1], lam[1])
    h_f = _linear_scan(a_f, b_f, h_cf[:, -1], False)
    h_b = _linear_scan(a_b, b_b, h_cb[:, 0], True)
    y_a = (h_f + h_b).astype(u.dtype) * jax.nn.gelu(ga)
    q = _apply_rope(q.reshape(B_, T, WIN_KV, WIN_G, WIN_HD), rope)
    k = _apply_rope(k.reshape(B_, T, WIN_KV, WIN_HD), rope)
    v = v.reshape(B_, T, WIN_KV, WIN_HD)
    kc = kc.reshape(B_, Lc, WIN_KV, WIN_HD)
    vc = vc.reshape(B_, Lc, WIN_KV, WIN_HD)
    sink_kg = sink.reshape(WIN_KV, WIN_G)
    o_b = _window_attention(q, k, v, kc, vc, sink_kg)
    y = jnp.concatenate([y_a, o_b], axis=-1) @ w_out
    if not with_ctx_out:
        return y, None
    y_ac = (h_cf + h_cb).astype(uc.dtype) * jax.nn.gelu(gac)
    o_bc = _gqa_attend(qc.reshape(B_, Lc, WIN_KV, WIN_G, WIN_HD), kc, vc, None, sink_kg).reshape(B_, Lc, -1)
    yc = jnp.concatenate([y_ac, o_bc], axis=-1) @ w_out
    return y, yc


def _odd_mixer(u, uc, w_in, q_gain, k_gain, w_out, rope, with_ctx_out):
    B_, T, _ = u.shape
    Lc = uc.shape[1]
    q, k, v = _split(u @ w_in, ODD_SPLITS)
    qc, kc, vc = _split(uc @ w_in, ODD_SPLITS)
    q = _apply_rope(_rmsnorm(q.reshape(B_, T, GLB_KV, GLB_G, GLB_HD), q_gain), rope)
    k = _apply_rope(_rmsnorm(k.reshape(B_, T, GLB_KV, GLB_HD), k_gain), rope)
    v = v.reshape(B_, T, GLB_KV, GLB_HD)
    kc = _rmsnorm(kc.reshape(B_, Lc, GLB_KV, GLB_HD), k_gain)
    vc = vc.reshape(B_, Lc, GLB_KV, GLB_HD)
    k_all = jnp.concatenate([kc, k], axis=1)
    v_all = jnp.concatenate([vc, v], axis=1)
    y = _dense_attention(q, k_all, v_all) @ w_out
    if not with_ctx_out:
        return y, None
    qc = _rmsnorm(qc.reshape(B_, Lc, GLB_KV, GLB_G, GLB_HD), q_gain)
    yc = _gqa_attend(qc, kc, vc, None, None).reshape(B_, Lc, -1) @ w_out
    return y, yc


def setup_inputs(seed: int = 0) -> dict:
    key = jax.random.key(seed)
    ks = jax.random.split(key, 32)
    D = D_MODEL

    def nrm(k, shape, scale):
        return jax.random.normal(k, shape, jnp.float32) * scale

    a0 = jax.random.uniform(ks[18], (N_EVEN, 2, LRU_WIDTH), jnp.float32, 0.9, 0.999)
    sig = a0 ** (1.0 / LRU_C)
    return {
        'x': nrm(ks[0], (BATCH, SEQ, D), 1.0),
        'c': nrm(ks[1], (BATCH, D), 1.0),
        'ctx': nrm(ks[2], (BATCH, CTX_LEN, D), 1.0),
        'c_ctx': nrm(ks[3], (D,), 1.0),
        'w_ada': nrm(ks[4], (DEPTH, D, N_MOD * D), 0.5 * D ** -0.5),
        'b_ada': nrm(ks[5], (DEPTH, N_MOD * D), 0.02),
        'norm_pre': 1.0 + nrm(ks[6], (DEPTH, N_SUB, D), 0.02),
        'norm_post': 1.0 + nrm(ks[7], (DEPTH, N_SUB, D), 0.02),
        'ffn_w_gate': nrm(ks[8], (DEPTH, 2, D, D_FF), D ** -0.5),
        'ffn_w_up': nrm(ks[9], (DEPTH, 2, D, D_FF), D ** -0.5),
        'ffn_w_down': nrm(ks[10], (DEPTH, 2, D_FF, D), D_FF ** -0.5),
        'even_w_in': nrm(ks[11], (N_EVEN, D, EVEN_IN), D ** -0.5),
        'even_conv_w': nrm(ks[12], (N_EVEN, CONV_W, LRU_WIDTH), CONV_W ** -0.5),
        'even_conv_b': nrm(ks[13], (N_EVEN, LRU_WIDTH), 0.02),
        'lru_w_a': nrm(ks[14], (N_EVEN, 2, LRU_BLOCKS, LRU_BW, LRU_BW), LRU_BW ** -0.5),
        'lru_b_a': nrm(ks[15], (N_EVEN, 2, LRU_WIDTH), 0.02),
        'lru_w_x': nrm(ks[16], (N_EVEN, 2, LRU_BLOCKS, LRU_BW, LRU_BW), LRU_BW ** -0.5),
        'lru_b_x': nrm(ks[17], (N_EVEN, 2, LRU_WIDTH), 0.02),
        'lru_lambda': jnp.log(sig) - jnp.log1p(-sig),
        'attn_sink': nrm(ks[19], (N_EVEN, WIN_HEADS), 0.5),
        'even_w_out': nrm(ks[20], (N_EVEN, EVEN_OUT, D), EVEN_OUT ** -0.5),
        'odd_w_in': nrm(ks[21], (N_ODD, D, ODD_IN), D ** -0.5),
        'odd_q_norm': 1.0 + nrm(ks[22], (N_ODD, GLB_HD), 0.02),
        'odd_k_norm': 1.0 + nrm(ks[23], (N_ODD, GLB_HD), 0.02),
        'odd_w_out': nrm(ks[24], (N_ODD, ODD_OUT, D), ODD_OUT ** -0.5),
    }


def reference(x, c, ctx, c_ctx, w_ada, b_ada, norm_pre, norm_post, ffn_w_gate, ffn_w_up, ffn_w_down,
              even_w_in, even_conv_w, even_conv_b, lru_w_a, lru_b_a, lru_w_x, lru_b_x, lru_lambda,
              attn_sink, even_w_out, odd_w_in, odd_q_norm, odd_k_norm, odd_w_out):
    B_, T, D = x.shape
    rows = T // GRID_W
    pos = jnp.arange(T)
    row = jnp.repeat(jnp.arange(rows), GRID_W)
    col = pos % GRID_W
    rope_win = _axial_rope(row, col, WIN_HD)
    rope_glb = _axial_rope(row, col, GLB_HD)
    xc = ctx
    sc = jax.nn.silu(c)
    scc = jax.nn.silu(c_ctx)
    for l in range(DEPTH):
        last = l == DEPTH - 1
        mod = (sc @ w_ada[l] + b_ada[l]).reshape(B_, N_MOD, 1, D)
        mod_c = (scc @ w_ada[l] + b_ada[l]).reshape(N_MOD, 1, D)
        x = _ffn_sublayer(x, mod[:, 0], mod[:, 1], mod[:, 2], norm_pre[l, 0], norm_post[l, 0],
                          ffn_w_gate[l, 0], ffn_w_up[l, 0], ffn_w_down[l, 0])
        xc = _ffn_sublayer(xc, mod_c[0], mod_c[1], mod_c[2], norm_pre[l, 0], norm_post[l, 0],
                           ffn_w_gate[l, 0], ffn_w_up[l, 0], ffn_w_down[l, 0])
        u = _modulate(_rmsnorm(x, norm_pre[l, 1]), mod[:, 3], mod[:, 4])
        uc = _modulate(_rmsnorm(xc, norm_pre[l, 1]), mod_c[3], mod_c[4])
        i = l // 2
        if l % 2 == 0:
            y, yc = _even_mixer(u, uc, even_w_in[i], even_conv_w[i], even_conv_b[i], lru_w_a[i], lru_b_a[i],
                                lru_w_x[i], lru_b_x[i], lru_lambda[i], attn_sink[i], even_w_out[i],
                                rope_win, not last)
        else:
            y, yc = _odd_mixer(u, uc, odd_w_in[i], odd_q_norm[i], odd_k_norm[i], odd_w_out[i],
                               rope_glb, not last)
        x = x + mod[:, 5] * _rmsnorm(y, norm_post[l, 1])
        x = _ffn_sublayer(x, mod[:, 6], mod[:, 7], mod[:, 8], norm_pre[l, 2], norm_post[l, 2],
                          ffn_w_gate[l, 1], ffn_w_up[l, 1], ffn_w_down[l, 1])
        if not last:
            xc = xc + mod_c[5] * _rmsnorm(yc, norm_post[l, 1])
            xc = _ffn_sublayer(xc, mod_c[6], mod_c[7], mod_c[8], norm_pre[l, 2], norm_post[l, 2],
                               ffn_w_gate[l, 1], ffn_w_up[l, 1], ffn_w_down[l, 1])
    return x
```

```python
import numpy as np
from contextlib import ExitStack
import concourse.bass as bass
import concourse.mybir as mybir
from concourse.bass_utils import run_bass_kernel_spmd

F32 = mybir.dt.float32
BF16 = mybir.dt.bfloat16
ALU = mybir.AluOpType
AF = mybir.ActivationFunctionType
AX = mybir.AxisListType

D = 1024
DFF = 2816
NJ = DFF // 128
DEPTH = 4
CTX = 256
SEQ = 8192
NB = 4
EPS = 1e-6


class Sched:
    EPOCH = 16000

    def __init__(self, nc, ctx):
        self.nc = nc
        self.ctx = ctx
        self.eng = {'pe': nc.tensor, 'act': nc.scalar, 'dve': nc.vector, 'pool': nc.gpsimd, 'sp': nc.sync}
        self.esem = {}
        self.ecount = {}
        self.sem_id = 0
        for e in self.eng:
            self._new_epoch(e)
        self.water = {}
        self.lastw = {}
        self.readers = {}
        self.dma_slots = {}
        self.dma_rr = {}
        self.n_instr = 0
        self.n_wait = 0

    def _new_sem(self, name):
        self.sem_id += 1
        return self.ctx.enter_context(self.nc.semaphore(f"{name}_{self.sem_id}"))

    def _new_epoch(self, e):
        self.esem[e] = self._new_sem("t" + e)
        self.ecount[e] = 0

    def _wait(self, e, tok):
        sem, val, src = tok
        if src == 'pe' and e == 'pe':
            return
        k = (e, id(sem))
        if self.water.get(k, 0) >= val:
            return
        self.water[k] = val
        self.eng[e].wait_ge(sem, val)
        self.n_wait += 1

    def _deps(self, e, reads, writes):
        toks = []
        for r in reads:
            t = self.lastw.get(r)
            if t is not None:
                toks.append(t)
        for w in writes:
            t = self.lastw.get(w)
            if t is not None:
                toks.append(t)
            toks.extend(self.readers.get(w, ()))
        best = {}
        for t in toks:
            k = id(t[0])
            if k not in best or best[k][1] < t[1]:
                best[k] = t
        for t in best.values():
            self._wait(e, t)

    def _record(self, tok, reads, writes):
        for r in reads:
            lst = self.readers.setdefault(r, [])
            for i, t in enumerate(lst):
                if t[0] is tok[0]:
                    lst[i] = tok
                    break
            else:
                lst.append(tok)
        for w in writes:
            self.lastw[w] = tok
            self.readers[w] = []

    def op(self, e, fn, reads=(), writes=()):
        self._deps(e, reads, writes)
        if self.ecount[e] >= self.EPOCH:
            self._new_epoch(e)
        ins = fn()
        self.ecount[e] += 1
        ins.then_inc(self.esem[e], 1)
        tok = (self.esem[e], self.ecount[e], e)
        self._record(tok, reads, writes)
        self.n_instr += 1
        return tok

    def dma(self, q, out, in_, reads=(), writes=(), nslots=12, **kw):
        self._deps(q, reads, writes)
        slots = self.dma_slots.setdefault(q, [])
        if len(slots) < nslots:
            slots.append([self._new_sem("d" + q), 0])
            i = len(slots) - 1
        else:
            i = self.dma_rr.get(q, 0) % nslots
            self.dma_rr[q] = i + 1
            self._wait(q, (slots[i][0], slots[i][1], 'dma'))
        slots[i][1] += 16
        self.eng[q].dma_start(out=out, in_=in_, **kw).then_inc(slots[i][0], 16)
        tok = (slots[i][0], slots[i][1], 'dma')
        self._record(tok, reads, writes)
        self.n_instr += 1
        return tok

    def all_tokens(self):
        toks = [(self.esem[e], self.ecount[e], e) for e in self.eng if self.ecount[e] > 0]
        for q, slots in self.dma_slots.items():
            for s in slots:
                toks.append((s[0], s[1], 'dma'))
        return toks

    def barrier(self):
        toks = self.all_tokens()
        for e in self.eng:
            for t in toks:
                self._wait(e, t)
        self.lastw.clear()
        self.readers.clear()


class Rot:
    def __init__(self, n):
        self.n = n
        self.i = -1

    def next(self):
        self.i += 1
        return self.i % self.n


class K:
    def __init__(self, seq=SEQ, stop_after=None, force_odd=False):
        self.force_odd = force_odd
        self.seq = seq
        self.ntok = CTX + seq
        self.ntile = self.ntok // 128
        self.stop_after = stop_after

    def sb(self, name, shape, dt, ctx=None):
        self._uid = getattr(self, '_uid', 0) + 1
        return (ctx or self.ctx).enter_context(self.nc.sbuf_tensor(f"{name}_{self._uid}", shape, dt))

    def stream_of(self, gt):
        return 1 if gt < CTX // 128 else 0

    def blocks(self, size_tiles, with_ctx=True):
        out = []
        if with_ctx:
            out.append((0, CTX // 128))
        t = CTX // 128
        while t < self.ntile:
            n = min(size_tiles, self.ntile - t)
            out.append((t, n))
            t += n
        return out

    def build(self):
        nc = bass.Bass("TRN2", target_bir_lowering=False)
        self.nc = nc
        seq, ntok = self.seq, self.ntok
        dt_in = lambda name, shape: nc.dram_tensor(name, shape, F32, kind="ExternalInput").ap()
        self.x_in = dt_in("x", [seq, D])
        self.ctx_in = dt_in("ctx", [CTX, D])
        self.ccT = dt_in("ccT", [128, 8, 2])
        self.w_ada = dt_in("w_ada", [DEPTH, D, 9 * D])
        self.b_adaT = dt_in("b_adaT", [128, DEPTH, 72])
        self.b_ada = dt_in("b_ada", [DEPTH, 9 * D])
        self.npreT = dt_in("npreT", [128, DEPTH, 3, 8])
        self.npost = dt_in("npost", [DEPTH, 3, D])
        self.wg = dt_in("wg", [DEPTH, 2, D, DFF])
        self.wu = dt_in("wu", [DEPTH, 2, D, DFF])
        self.wd = dt_in("wd", [DEPTH, 2, DFF, D])
        self.owin = dt_in("owin", [2, D, 1536])
        self.owout = dt_in("owout", [2, D, D])
        self.oqn = dt_in("oqn", [2, 128])
        self.okn = dt_in("okn", [2, 128])
        self.ropeAg = dt_in("ropeAg", [ntok, 128])
        self.ropeBg = dt_in("ropeBg", [ntok, 128])
        self.ropeAw = dt_in("ropeAw", [ntok, 64])
        self.ropeBw = dt_in("ropeBw", [ntok, 64])
        self.ewin = dt_in("ewin", [2, D, 1792])
        self.ewout = dt_in("ewout", [2, D, D])
        self.bdw = dt_in("bdw", [2, 16, 128, 128])
        self.evec = dt_in("evec", [2, 128, 4, 11])
        self.sinkrep = dt_in("sinkrep", [2, 1024])
        self.masks = dt_in("masks", [2, 128, 512])
        self.XAd = nc.dram_tensor("XAd", [512, ntok + 6], F32, kind="Internal").ap()
        self.GGd = nc.dram_tensor("GGd", [512, ntok], F32, kind="Internal").ap()
        self.HFd = nc.dram_tensor("HFd", [512, ntok], F32, kind="Internal").ap()
        self.ABd = nc.dram_tensor("ABd", [512, ntok], F32, kind="Internal").ap()
        self.BBd = nc.dram_tensor("BBd", [512, ntok], F32, kind="Internal").ap()
        self.Qd = nc.dram_tensor("Qd", [ntok, 512], BF16, kind="Internal").ap()
        self.out = nc.dram_tensor("out", [seq, D], F32, kind="ExternalOutput").ap()
        self.X = nc.dram_tensor("Xs", [ntok, D], F32, kind="Internal").ap()
        self.Cd = nc.dram_tensor("Cd", [DEPTH, 3, 2, D], F32, kind="Internal").ap()

        with ExitStack() as ctx:
            self.ctx = ctx
            self.S = S = Sched(nc, ctx)
            self.ps = ctx.enter_context(nc.psum_tensor("ps", [128, 8, 512], F32))
            self.modT = self.sb("modT", [128, DEPTH, 72, 2], F32)
            self.AT = self.sb("AT", [128, DEPTH, 3, 8, 2], F32)
            self.identf = self.sb("identf", [128, 128], F32)
            self.ident = self.sb("ident", [128, 128], BF16)
            self.ones_bf = self.sb("ones_bf", [128, 128], BF16)
            self.eps_t = self.sb("eps_t", [128, 1], F32)
            self.one_t = self.sb("one_t", [128, 1], F32)
            self.junk = self.sb("junk", [128, 1024], BF16)
            ctx.enter_context(nc.Block())
            S.op('pool', lambda: nc.gpsimd.memset(self.identf[:], 1.0), writes=['identf'])
            S.op('pool', lambda: nc.gpsimd.affine_select(out=self.identf[:], in_=self.identf[:], pattern=[[-1, 128]],
                                                          compare_op=ALU.is_equal, fill=0.0, base=0, channel_multiplier=1),
                 reads=['identf'], writes=['identf'])
            S.op('dve', lambda: nc.vector.tensor_copy(out=self.ident[:], in_=self.identf[:]), reads=['identf'], writes=['ident'])
            S.op('pool', lambda: nc.gpsimd.memset(self.ones_bf[:], 1.0), writes=['ones_bf'])
            S.op('pool', lambda: nc.gpsimd.memset(self.eps_t[:], EPS), writes=['eps_t'])
            S.op('pool', lambda: nc.gpsimd.memset(self.one_t[:], 1.0), writes=['one_t'])

            S.dma('sp', self.X[0:CTX, :], self.ctx_in[:, :], writes=[('X', t) for t in range(CTX // 128)])
            step = 1024
            for r0 in range(0, seq, step):
                S.dma('sp', self.X[CTX + r0:CTX + r0 + step, :], self.x_in[r0:r0 + step, :],
                      writes=[('X', (CTX + r0) // 128 + t) for t in range(step // 128)])

            self.prologue()
            S.barrier()
            nsub = 0
            done = False
            for l in range(DEPTH):
                last = (l == DEPTH - 1)
                for sub in range(3):
                    if self.stop_after is not None and nsub >= self.stop_after:
                        done = True
                        break
                    final = last and sub == 2
                    if sub in (0, 2):
                        self.ffn(l, sub, with_ctx=not (last and sub == 2), final=final)
                    else:
                        self.mixer(l)
                    S.barrier()
                    nsub += 1
                if done:
                    break
            if done or True:
                pass
            if self.stop_after is not None:
                toks = []
                for r0 in range(0, seq, 1024):
                    toks.append(S.dma('sp', self.out[r0:r0 + 1024, :], self.X[CTX + r0:CTX + r0 + 1024, :],
                                      reads=[('X', (CTX + r0) // 128 + t) for t in range(8)], writes=[('out', r0)]))
            for t in S.all_tokens():
                S._wait('sp', t)
            self.n_instr = S.n_instr
            self.n_wait = S.n_wait
        return nc

    def prologue(self):
        nc, S, ps = self.nc, self.S, self.ps
        with ExitStack() as pc:
            scT = self.sb("scT", [128, 8, 2], F32, pc)
            badaT = self.sb("badaT", [128, DEPTH, 72], F32, pc)
            npreT = self.sb("npreT_s", [128, DEPTH, 3, 8], F32, pc)
            slab = [self.sb(f"adaslab{i}", [128, 8, 1024], F32, pc) for i in range(2)]
            brow = [self.sb(f"brow{i}", [2, 1024], F32, pc) for i in range(2)]
            grow = [self.sb(f"grow{i}", [2, 1024], F32, pc) for i in range(2)]
            crow = [self.sb(f"crow{i}", [2, 1024], F32, pc) for i in range(2)]
            crow2 = [self.sb(f"crow2{i}", [2, 1024], F32, pc) for i in range(2)]
            S.dma('sp', scT[:], self.ccT[:, :, :], writes=['scT'])
            S.dma('sp', badaT[:], self.b_adaT[:, :, :], writes=['badaT'])
            S.dma('sp', npreT[:], self.npreT[:, :, :, :], writes=['npreT'])
            S.op('act', lambda: nc.scalar.activation(out=scT[:], in_=scT[:], func=AF.Silu), reads=['scT'], writes=['scT'])
            rr = Rot(2)
            r2 = Rot(2)
            for l in range(DEPTH):
                for m in range(9):
                    si = rr.next()
                    wv = self.w_ada[l].rearrange("(c p) n -> p c n", p=128)
                    S.dma('sp', slab[si][:, 0:4, :], wv[:, 0:4, m * 1024:(m + 1) * 1024], writes=[('slab', si)])
                    S.dma('pool', slab[si][:, 4:8, :], wv[:, 4:8, m * 1024:(m + 1) * 1024], writes=[('slabB', si)])
                    for h in range(2):
                        for kc in range(8):
                            S.op('pe', lambda: nc.tensor.matmul(ps[0:2, 1 + h, :], lhsT=scT[:, kc, :],
                                                                rhs=slab[si][:, kc, h * 512:(h + 1) * 512],
                                                                start=(kc == 0), stop=(kc == 7)),
                                 reads=[('slab', si), ('slabB', si), 'scT'], writes=[('ps', 1 + h)])
                    ri = r2.next()
                    S.dma('sp', brow[ri][:], self.b_ada[l:l + 1, m * 1024:(m + 1) * 1024].partition_broadcast(2),
                          writes=[('brow', ri)])
                    S.op('dve', lambda: nc.vector.tensor_tensor(out=crow[ri][:], in0=ps[0:2, 1:3, :].rearrange("p a b -> p (a b)"),
                                                                in1=brow[ri][:], op=ALU.add),
                         reads=[('ps', 1), ('ps', 2), ('brow', ri)], writes=[('crow', ri)])
                    for c in range(8):
                        j = m * 8 + c
                        S.op('pe', lambda: nc.tensor.transpose(out=ps[:, 0, j * 2:j * 2 + 2], in_=crow[ri][0:2, c * 128:(c + 1) * 128],
                                                               identity=self.identf[0:2, 0:2]),
                             reads=[('crow', ri), 'identf'], writes=[('ps', 0)])
                    if m in (2, 5, 8):
                        sub = m // 3
                        S.dma('sp', grow[ri][:], self.npost[l, sub:sub + 1, :].partition_broadcast(2), writes=[('grow', ri)])
                        S.op('dve', lambda: nc.vector.scalar_tensor_tensor(out=crow2[ri][:], in0=crow[ri][:],
                                                                           scalar=(1.0 if sub == 1 else 0.5), in1=grow[ri][:],
                                                                           op0=ALU.mult, op1=ALU.mult),
                             reads=[('crow', ri), ('grow', ri)], writes=[('crow2', ri)])
                        S.dma('sp', self.Cd[l, sub, :, :], crow2[ri][:], reads=[('crow2', ri)], writes=[('Cd', l, sub)])
                S.op('dve', lambda: nc.vector.tensor_copy(out=self.modT[:, l, :, :],
                                                          in_=ps[:, 0, 0:144].rearrange("p (j s) -> p j s", s=2)),
                     reads=[('ps', 0)], writes=['modT'])
                for sub in range(3):
                    m = 3 * sub + 1
                    S.op('dve', lambda: nc.vector.scalar_tensor_tensor(
                        out=self.AT[:, l, sub, :, :], in0=self.modT[:, l, m * 8:(m + 1) * 8, :], scalar=1.0,
                        in1=npreT[:, l, sub, :].unsqueeze(2).to_broadcast([128, 8, 2]), op0=ALU.add, op1=ALU.mult),
                        reads=['modT', 'npreT'], writes=['AT'])
            S.barrier()

    def prep_alloc(self, pc):
        self.xin = [self.sb(f"xin{i}", [128, 1024], F32, pc) for i in range(2)]
        self.xn = [self.sb(f"xn{i}", [128, 1024], BF16, pc) for i in range(2)]
        self.ssb = self.sb("ssb", [128, 8], F32, pc)
        self.rx = Rot(2)
        self.rn = Rot(2)
        self.rs = Rot(8)
        self.rtp = Rot(2)

    def prep_tile(self, l, sub, gt, uT_dst, uT_res):
        nc, S, ps = self.nc, self.S, self.ps
        s = self.stream_of(gt)
        xi = self.rx.next()
        S.dma('sp', self.xin[xi][:], self.X[gt * 128:(gt + 1) * 128, :], reads=[('X', gt)], writes=[('xin', xi)])
        si = self.rs.next()
        ssc = self.ssb[:, si:si + 1]
        S.op('act', lambda: nc.scalar.activation(out=self.junk[:], in_=self.xin[xi][:], func=AF.Square, accum_out=ssc),
             reads=[('xin', xi)], writes=[('ss', si)])
        S.op('act', lambda: nc.scalar.activation(out=ssc, in_=ssc, func=AF.Sqrt, scale=1.0 / D, bias=self.eps_t[:]),
             reads=[('ss', si), 'eps_t'], writes=[('ss', si)])
        S.op('dve', lambda: nc.vector.reciprocal(out=ssc, in_=ssc), reads=[('ss', si)], writes=[('ss', si)])
        ni = self.rn.next()
        S.op('act', lambda: nc.scalar.activation(out=self.xn[ni][:], in_=self.xin[xi][:], func=AF.Identity, scale=ssc),
             reads=[('xin', xi), ('ss', si)], writes=[('xn', ni)])
        bk = 6 + self.rtp.next()
        pT = ps[:, bk, :].bitcast(BF16)
        for c in range(8):
            S.op('pe', lambda: nc.tensor.transpose(out=pT[:, c * 128:(c + 1) * 128], in_=self.xn[ni][:, c * 128:(c + 1) * 128],
                                                   identity=self.ident[:]),
                 reads=[('xn', ni), 'ident'], writes=[('ps', bk)])
        for c in range(8):
            mb = 3 * sub
            S.op('dve', lambda: nc.vector.tensor_scalar(out=uT_dst[:, c, :], in0=pT[:, c * 128:(c + 1) * 128],
                                                         scalar1=self.AT[:, l, sub, c, s:s + 1],
                                                         scalar2=self.modT[:, l, mb * 8 + c, s:s + 1],
                                                         op0=ALU.mult, op1=ALU.add),
                 reads=[('ps', bk), 'AT', 'modT'], writes=[uT_res])

    def epi_alloc(self, pc):
        self.CB = [self.sb(f"CB{i}", [128, 1024], F32, pc) for i in range(2)]
        self.xr = [self.sb(f"xr{i}", [128, 1024], F32, pc) for i in range(1)]
        self.tb = [self.sb(f"tb{i}", [128, 1024], F32, pc) for i in range(1)]
        self.ob = [self.sb(f"ob{i}", [128, 1024], F32, pc) for i in range(2)]
        self.ss2 = self.sb("ss2", [128, 8], F32, pc)
        self.rxr = Rot(1)
        self.rtb = Rot(1)
        self.rob = Rot(2)
        self.rs2 = Rot(8)

    def epi_load_cb(self, l, sub):
        for s in range(2):
            self.S.dma('sp', self.CB[s][:], self.Cd[l, sub, s:s + 1, :].partition_broadcast(128),
                       reads=[('Cd', l, sub)], writes=[('CB', s)])

    def epilogue(self, gt, ybank, final=False):
        nc, S, ps = self.nc, self.S, self.ps
        s = self.stream_of(gt)
        yp = ps[:, ybank:ybank + 2, :].rearrange("p a b -> p (a b)")
        yres = [('ps', ybank), ('ps', ybank + 1)]
        xi = self.rxr.next()
        S.dma('sp', self.xr[xi][:], self.X[gt * 128:(gt + 1) * 128, :], reads=[('X', gt)], writes=[('xr', xi)])
        si = self.rs2.next()
        ssc = self.ss2[:, si:si + 1]
        S.op('act', lambda: nc.scalar.activation(out=self.junk[:], in_=yp, func=AF.Square, accum_out=ssc),
             reads=yres, writes=[('ss2', si)])
        S.op('act', lambda: nc.scalar.activation(out=ssc, in_=ssc, func=AF.Sqrt, scale=1.0 / D, bias=self.eps_t[:]),
             reads=[('ss2', si), 'eps_t'], writes=[('ss2', si)])
        S.op('dve', lambda: nc.vector.reciprocal(out=ssc, in_=ssc), reads=[('ss2', si)], writes=[('ss2', si)])
        ti = self.rtb.next()
        S.op('dve', lambda: nc.vector.tensor_tensor(out=self.tb[ti][:], in0=yp, in1=self.CB[s][:], op=ALU.mult),
             reads=yres + [('CB', s)], writes=[('tb', ti)])
        oi = self.rob.next()
        S.op('dve', lambda: nc.vector.scalar_tensor_tensor(out=self.ob[oi][:], in0=self.tb[ti][:], scalar=ssc, in1=self.xr[xi][:],
                                                           op0=ALU.mult, op1=ALU.add),
             reads=[('tb', ti), ('ss2', si), ('xr', xi)], writes=[('ob', oi)])
        if final:
            lt = gt - CTX // 128
            S.dma('sp', self.out[lt * 128:(lt + 1) * 128, :], self.ob[oi][:], reads=[('ob', oi)], writes=[('out', lt)])
        else:
            S.dma('sp', self.X[gt * 128:(gt + 1) * 128, :], self.ob[oi][:], reads=[('ob', oi)], writes=[('X', gt)])

    def ffn(self, l, sub, with_ctx=True, final=False):
        nc, S, ps = self.nc, self.S, self.ps
        f = 0 if sub == 0 else 1
        with ExitStack() as pc:
            self.prep_alloc(pc)
            self.epi_alloc(pc)
            wd_sb = self.sb("wd_sb", [128, NJ, 1024], BF16, pc)
            uT = [self.sb(f"uT{i}", [128, 8, 1024], BF16, pc) for i in range(2)]
            actT = self.sb("actT", [128, NJ, 1024], BF16, pc)
            NSL = 3
            wgs = [self.sb(f"wgs{i}", [128, 8, 256], BF16, pc) for i in range(NSL)]
            wus = [self.sb(f"wus{i}", [128, 8, 256], BF16, pc) for i in range(NSL)]
            sg = [self.sb(f"sg{i}", [128, 512], F32, pc) for i in range(2)]
            wgv = self.wg[l, f].rearrange("(c p) n -> p c n", p=128)
            wuv = self.wu[l, f].rearrange("(c p) n -> p c n", p=128)
            wdv = self.wd[l, f].rearrange("(j p) n -> p j n", p=128)
            self.epi_load_cb(l, sub)
            for j0 in range(0, NJ, 6):
                j1 = min(NJ, j0 + 6)
                S.dma('pool', wd_sb[:, j0:j1, :], wdv[:, j0:j1, :], writes=[('wd', j) for j in range(j0, j1)])
            blocks = self.blocks(8, with_ctx)
            rsl = Rot(NSL)
            rp1 = Rot(2)
            rsg = Rot(2)
            ry = Rot(2)

            def prep_block(bi):
                t0, nt = blocks[bi]
                par = bi % 2
                return [(lambda tt=tt: self.prep_tile(l, sub, t0 + tt, uT[par][:, :, tt * 128:(tt + 1) * 128], ('uT', par, tt)))
                        for tt in range(nt)]

            for fn in prep_block(0):
                fn()
            for bi, (t0, nt) in enumerate(blocks):
                par = bi % 2
                pending = prep_block(bi + 1) if bi + 1 < len(blocks) else []
                ntk = nt * 128
                halves = [(h0, min(512, ntk - h0)) for h0 in range(0, ntk, 512)]
                for js in range(0, NJ, 2):
                    sl = rsl.next()
                    S.dma('pool', wgs[sl][:], wgv[:, :, js * 128:js * 128 + 256], writes=[('wgs', sl)])
                    S.dma('pool', wus[sl][:], wuv[:, :, js * 128:js * 128 + 256], writes=[('wus', sl)])
                    for jj in range(2):
                        j = js + jj
                        for (h0, n) in halves:
                            pp = rp1.next()
                            bg, bu = 2 * pp, 2 * pp + 1
                            ures = [('uT', par, tt) for tt in range(h0 // 128, (h0 + n) // 128)]
                            for kc in range(8):
                                S.op('pe', lambda: nc.tensor.matmul(ps[:, bg, 0:n], lhsT=wgs[sl][:, kc, jj * 128:(jj + 1) * 128],
                                                                    rhs=uT[par][:, kc, h0:h0 + n], start=(kc == 0), stop=(kc == 7)),
                                     reads=[('wgs', sl)] + ures, writes=[('ps', bg)])
                            for kc in range(8):
                                S.op('pe', lambda: nc.tensor.matmul(ps[:, bu, 0:n], lhsT=wus[sl][:, kc, jj * 128:(jj + 1) * 128],
                                                                    rhs=uT[par][:, kc, h0:h0 + n], start=(kc == 0), stop=(kc == 7)),
                                     reads=[('wus', sl)] + ures, writes=[('ps', bu)])
                            gi = rsg.next()
                            S.op('act', lambda: nc.scalar.activation(out=sg[gi][:, 0:n], in_=ps[:, bg, 0:n], func=AF.Silu),
                                 reads=[('ps', bg)], writes=[('sg', gi)])
                            S.op('dve', lambda: nc.vector.tensor_tensor(out=actT[:, j, h0:h0 + n], in0=sg[gi][:, 0:n],
                                                                        in1=ps[:, bu, 0:n], op=ALU.mult),
                                 reads=[('sg', gi), ('ps', bu)], writes=[('act', j, h0)])
                    if pending and js >= 2:
                        pending.pop(0)()
                while pending:
                    pending.pop(0)()
                for tt in range(nt):
                    yb = 4 + 2 * ry.next()
                    h0 = (tt * 128) // 512 * 512
                    for dh in range(2):
                        for j in range(NJ):
                            S.op('pe', lambda: nc.tensor.matmul(ps[:, yb + dh, :], lhsT=actT[:, j, tt * 128:(tt + 1) * 128],
                                                                rhs=wd_sb[:, j, dh * 512:(dh + 1) * 512],
                                                                start=(j == 0), stop=(j == NJ - 1)),
                                 reads=[('act', j, h0), ('wd', j)], writes=[('ps', yb + dh)])
                    self.epilogue(t0 + tt, yb, final=final)

    def mixer(self, l):
        if l % 2 == 1 or self.force_odd:
            self.odd_mixer(l)
        else:
            self.even_mixer(l)

    def normrope(self, src_ps, src_res, H, hd, gain, ra, rb, rope_res, dst_bf, dst_res, do_norm):
        nc, S = self.nc, self.S
        W = H * hd
        t = self.nrt[:, 0:W]
        v3 = lambda ap: ap.rearrange("p (h d) -> p h d", h=H)
        v4 = lambda ap: ap.rearrange("p (h i two) -> p h i two", h=H, two=2)
        if do_norm:
            sq = self.tb[0][:, 0:W]
            S.op('act', lambda: nc.scalar.activation(out=sq, in_=src_ps, func=AF.Square), reads=src_res, writes=[('tb', 0)])
            k0 = 8 * self.rsh.next()
            ssh = self.ssh[:, k0:k0 + H]
            S.op('dve', lambda: nc.vector.tensor_reduce(out=ssh, in_=v3(sq), axis=AX.X, op=ALU.add),
                 reads=[('tb', 0)], writes=[('ssh', k0)])
            S.op('act', lambda: nc.scalar.activation(out=ssh, in_=ssh, func=AF.Sqrt, scale=1.0 / hd, bias=self.eps_t[:]),
                 reads=[('ssh', k0), 'eps_t'], writes=[('ssh', k0)])
            S.op('dve', lambda: nc.vector.reciprocal(out=ssh, in_=ssh), reads=[('ssh', k0)], writes=[('ssh', k0)])
            S.op('dve', lambda: nc.vector.tensor_tensor(out=v3(t), in0=v3(src_ps), in1=ssh.unsqueeze(2).to_broadcast([128, H, hd]),
                                                        op=ALU.mult), reads=src_res + [('ssh', k0)], writes=['nrt'])
            S.op('pool', lambda: nc.gpsimd.tensor_tensor(out=v3(t), in0=v3(t), in1=gain[:].unsqueeze(1).to_broadcast([128, H, hd]),
                                                         op=ALU.mult), reads=['nrt', 'gains'], writes=['nrt'])
        else:
            S.op('act', lambda: nc.scalar.copy(out=t, in_=src_ps), reads=src_res, writes=['nrt'])
        A = self.xr[0][:, 0:W]
        B = self.ob[0][:, 0:W]
        S.op('dve', lambda: nc.vector.tensor_tensor(out=v3(A), in0=v3(t), in1=ra.unsqueeze(1).to_broadcast([128, H, hd]), op=ALU.mult),
             reads=['nrt', rope_res], writes=[('xr', 0)])
        rb3 = rb.rearrange("p (i two) -> p i two", two=2)
        S.op('pool', lambda: nc.gpsimd.tensor_tensor(out=v4(B), in0=v4(t)[:, :, :, ::-1],
                                                     in1=rb3.unsqueeze(1).to_broadcast([128, H, hd // 2, 2]), op=ALU.mult),
             reads=['nrt', rope_res], writes=[('ob', 0)])
        S.op('dve', lambda: nc.vector.tensor_tensor(out=dst_bf, in0=A, in1=B, op=ALU.add),
             reads=[('xr', 0), ('ob', 0)], writes=[dst_res])

    def load_rope(self, gt, tabA, tabB, hd):
        ri = self.rrope.next()
        self.S.dma('sp', self.ropeA[ri][:, 0:hd], tabA[gt * 128:(gt + 1) * 128, :], writes=[('rope', ri)])
        self.S.dma('sp', self.ropeB[ri][:, 0:hd], tabB[gt * 128:(gt + 1) * 128, :], writes=[('rope', ri)])
        return ri

    def mix_alloc(self, pc):
        self.prep_alloc(pc)
        self.epi_alloc(pc)
        self.nrt = self.sb("nrt", [128, 1024], F32, pc)
        self.ssh = self.sb("ssh", [128, 32], F32, pc)
        self.rsh = Rot(4)
        self.ropeA = [self.sb(f"ropeA{i}", [128, 128], F32, pc) for i in range(2)]
        self.ropeB = [self.sb(f"ropeB{i}", [128, 128], F32, pc) for i in range(2)]
        self.rrope = Rot(2)
        self.uTt = [self.sb(f"uTt{i}", [128, 8, 128], BF16, pc) for i in range(2)]
        self.rut = Rot(2)
        self.qr = self.sb("qr", [128, 1024], BF16, pc)
        self.negshift = self.sb("negshift", [128, 1], F32, pc)
        self.S.op('pool', lambda: self.nc.gpsimd.memset(self.negshift[:], -4.0), writes=['negshift'])

    def odd_mixer(self, l):
        nc, S, ps = self.nc, self.S, self.ps
        i = l // 2
        last = (l == DEPTH - 1)
        NT = self.ntile
        with ExitStack() as pc:
            self.mix_alloc(pc)
            win = self.sb("win", [128, 8, 1536], BF16, pc)
            wout = self.sb("wout", [128, 8, 1024], BF16, pc)
            KT = self.sb("KT", [128, 2, NT * 128], BF16, pc)
            V = self.sb("V", [128, NT, 2, 130], BF16, pc)
            Otok = self.sb("Otok", [128, 8, 4, 128], BF16, pc)
            rden4 = self.sb("rden4", [128, 4], F32, pc)
            QT = self.sb("QT", [128, 8, 512], BF16, pc)
            OT = self.sb("OT", [128, 8, 512], BF16, pc)
            pbuf = [self.sb(f"pbuf{i}", [128, 2, 512], BF16, pc) for i in range(2)]
            gq = self.sb("gq", [128, 128], F32, pc)
            gk = self.sb("gk", [128, 128], F32, pc)
            S.dma('pool', win[:], self.owin[i].rearrange("(c p) n -> p c n", p=128), writes=['win'])
            S.dma('pool', wout[:], self.owout[i].rearrange("(c p) n -> p c n", p=128), writes=['wout'])
            S.dma('sp', gq[:], self.oqn[i:i + 1, :].partition_broadcast(128), writes=['gains'])
            S.dma('sp', gk[:], self.okn[i:i + 1, :].partition_broadcast(128), writes=['gains'])
            self.epi_load_cb(l, 1)
            S.op('pool', lambda: nc.gpsimd.memset(V[:, :, :, 128:130], 1.0), writes=['Vones'])
            rkv = Rot(2)
            for gt in range(NT):
                ui = self.rut.next()
                self.prep_tile(l, 1, gt, self.uTt[ui], ('uTt', ui))
                ri = self.load_rope(gt, self.ropeAg, self.ropeBg, 128)
                bk = 4 + rkv.next()
                for kc in range(8):
                    S.op('pe', lambda: nc.tensor.matmul(ps[:, bk, :], lhsT=self.uTt[ui][:, kc, :], rhs=win[:, kc, 1024:1536],
                                                        start=(kc == 0), stop=(kc == 7)),
                         reads=[('uTt', ui), 'win'], writes=[('ps', bk)])
                S.op('act', lambda: nc.scalar.copy(out=V[:, gt, :, 0:128], in_=ps[:, bk, 256:512].rearrange("p (k d) -> p k d", k=2)),
                     reads=[('ps', bk)], writes=[('V', gt)])
                self.normrope(ps[:, bk, 0:256], [('ps', bk)], 2, 128, gk, self.ropeA[ri][:, 0:128], self.ropeB[ri][:, 0:128],
                              ('rope', ri), self.qr[:, 0:256], 'qr', True)
                tb_ = 6 + self.rtp.next()
                pT = ps[:, tb_, :].bitcast(BF16)
                for h in range(2):
                    S.op('pe', lambda: nc.tensor.transpose(out=pT[:, h * 128:(h + 1) * 128], in_=self.qr[:, h * 128:(h + 1) * 128],
                                                           identity=self.ident[:]), reads=['qr', 'ident'], writes=[('ps', tb_)])
                S.op('act', lambda: nc.scalar.copy(out=KT[:, :, gt * 128:(gt + 1) * 128],
                                                   in_=pT[:, 0:256].rearrange("p (h t) -> p h t", h=2)),
                     reads=[('ps', tb_)], writes=[('KT', gt)])
            qblocks = self.blocks(4, with_ctx=not last)
            rS = Rot(2)
            rO = Rot(2)
            rP = Rot(2)
            ry = Rot(2)
            sc = 128.0 ** -0.5
            for (t0, nt) in qblocks:
                nq = nt * 128
                is_ctx = t0 < CTX // 128
                for tl in range(nt):
                    gt = t0 + tl
                    ui = self.rut.next()
                    self.prep_tile(l, 1, gt, self.uTt[ui], ('uTt', ui))
                    ri = self.load_rope(gt, self.ropeAg, self.ropeBg, 128)
                    for half in range(2):
                        for kc in range(8):
                            S.op('pe', lambda: nc.tensor.matmul(ps[:, 4 + half, :], lhsT=self.uTt[ui][:, kc, :],
                                                                rhs=win[:, kc, half * 512:(half + 1) * 512],
                                                                start=(kc == 0), stop=(kc == 7)),
                                 reads=[('uTt', ui), 'win'], writes=[('ps', 4 + half)])
                    self.normrope(ps[:, 4:6, :].rearrange("p a b -> p (a b)"), [('ps', 4), ('ps', 5)], 8, 128, gq,
                                  self.ropeA[ri][:, 0:128], self.ropeB[ri][:, 0:128], ('rope', ri), self.qr[:, :], 'qr', True)
                    tb_ = 6 + self.rtp.next()
                    pT = ps[:, tb_, :].bitcast(BF16)
                    for h in range(8):
                        S.op('pe', lambda: nc.tensor.transpose(out=pT[:, h * 128:(h + 1) * 128], in_=self.qr[:, h * 128:(h + 1) * 128],
                                                               identity=self.ident[:]), reads=['qr', 'ident'], writes=[('ps', tb_)])
                    S.op('act', lambda: nc.scalar.copy(out=QT[:, :, tl * 128:(tl + 1) * 128],
                                                       in_=pT.rearrange("p (h t) -> p h t", h=8)),
                         reads=[('ps', tb_)], writes=['QT'])
                chunks = list(range(CTX // 128)) if is_ctx else list(range(NT))
                nqt = nq // 128
                steps = [(h, c) for h in range(8) for c in chunks]
                rS4 = Rot(4)
                rP4 = Rot(4)

                def emit_S(st):
                    h, c = st
                    kv = h // 4
                    bk_ = rS4.next()
                    S.op('pe', lambda: nc.tensor.matmul(ps[:, bk_, 0:nq], lhsT=KT[:, kv, c * 128:(c + 1) * 128],
                                                        rhs=QT[:, h, 0:nq], start=True, stop=True),
                         reads=[('KT', c), 'QT'], writes=[('ps', bk_)])
                    return bk_

                def emit_rest(st, bk_):
                    h, c = st
                    kv = h // 4
                    pi = rP4.next()
                    pb = pbuf[pi // 2][:, pi % 2, :]
                    S.op('act', lambda: nc.scalar.activation(out=pb[:, 0:nq], in_=ps[:, bk_, 0:nq],
                                                             func=AF.Exp, scale=sc, bias=self.negshift[:]),
                         reads=[('ps', bk_), 'negshift'], writes=[('pbuf', pi)])
                    first = (c == chunks[0])
                    lastc = (c == chunks[-1])
                    for qt in range(nqt):
                        S.op('pe', lambda: nc.tensor.matmul(ps[:, 4 + qt, 0:130], lhsT=pb[:, qt * 128:(qt + 1) * 128],
                                                            rhs=V[:, c, kv, :], start=first, stop=lastc),
                             reads=[('V', c), 'Vones', ('pbuf', pi)], writes=[('ps', 4 + qt)])
                    if lastc:
                        ores = [('ps', 4 + qt) for qt in range(nqt)]
                        S.op('dve', lambda: nc.vector.reciprocal(out=rden4[:, 0:nqt].unsqueeze(2), in_=ps[:, 4:4 + nqt, 128:129]),
                             reads=ores, writes=['rden4'])
                        S.op('dve', lambda: nc.vector.tensor_tensor(out=Otok[:, h, 0:nqt, :], in0=ps[:, 4:4 + nqt, 0:128],
                                                                    in1=rden4[:, 0:nqt].unsqueeze(2).to_broadcast([128, nqt, 128]),
                                                                    op=ALU.mult),
                             reads=ores + ['rden4'], writes=[('Otok', h)])

                PD = 3
                pend = []
                nxt = 0
                while nxt < len(steps) or pend:
                    while nxt < len(steps) and len(pend) < PD + 1:
                        pend.append((steps[nxt], emit_S(steps[nxt])))
                        nxt += 1
                    st, bk_ = pend.pop(0)
                    emit_rest(st, bk_)
                for h in range(8):
                    bk_ = h % 4
                    pT = ps[:, bk_, :].bitcast(BF16)
                    for qt in range(nqt):
                        S.op('pe', lambda: nc.tensor.transpose(out=pT[:, qt * 128:(qt + 1) * 128], in_=Otok[:, h, qt, :],
                                                               identity=self.ident[:]), reads=[('Otok', h), 'ident'], writes=[('ps', bk_)])
                    if h % 2 == 0:
                        S.op('act', lambda: nc.scalar.copy(out=OT[:, h, 0:nq], in_=pT[:, 0:nq]), reads=[('ps', bk_)], writes=[('OT', h)])
                    else:
                        S.op('dve', lambda: nc.vector.tensor_copy(out=OT[:, h, 0:nq], in_=pT[:, 0:nq]), reads=[('ps', bk_)], writes=[('OT', h)])
                for tl in range(nt):
                    yb = 2 * ry.next()
                    for dh in range(2):
                        for h in range(8):
                            S.op('pe', lambda: nc.tensor.matmul(ps[:, yb + dh, :], lhsT=OT[:, h, tl * 128:(tl + 1) * 128],
                                                                rhs=wout[:, h, dh * 512:(dh + 1) * 512], start=(h == 0), stop=(h == 7)),
                                 reads=[('OT', h), 'wout'], writes=[('ps', yb + dh)])
                    self.epilogue(t0 + tl, yb)

    def even_mixer(self, l):
        nc, S, ps = self.nc, self.S, self.ps
        i = l // 2
        NT = self.ntile
        NC_ = CTX // 128
        seq, ntok = self.seq, self.ntok
        col_of = lambda n: (2 + n) if n < CTX else (5 + n)
        with ExitStack() as pc:
            self.mix_alloc(pc)
            win = self.sb("ewin", [128, 8, 1792], BF16, pc)
            woutA = self.sb("ewoutA", [128, 4, 1024], BF16, pc)
            woutB = self.sb("ewoutB", [64, 8, 1024], BF16, pc)
            KTw = self.sb("KTw", [64, 2, NT * 128], BF16, pc)
            Vw = self.sb("Vw", [128, NT, 128], BF16, pc)
            BD = self.sb("BD", [128, 16, 128], BF16, pc)
            evec = self.sb("evec", [128, 4, 11], F32, pc)
            csT = self.sb("csT", [128, 4, 2], F32, pc)
            mlo = self.sb("mlo", [128, 512], BF16, pc)
            mhi = self.sb("mhi", [128, 512], BF16, pc)
            esk = self.sb("esk", [1, 1024], F32, pc)
            eskb = self.sb("eskb", [1, 1024], BF16, pc)
            zt = self.sb("zt", [128, 4], F32, pc)
            stf = self.sb("stf", [128, 4], F32, pc)
            stb = self.sb("stb", [128, 4], F32, pc)
            S.dma('pool', win[:], self.ewin[i].rearrange("(c p) n -> p c n", p=128), writes=['win'])
            S.dma('pool', woutA[:], self.ewout[i, 0:512, :].rearrange("(c p) n -> p c n", p=128), writes=['wout'])
            S.dma('pool', woutB[:], self.ewout[i, 512:1024, :].rearrange("(h d) n -> d h n", d=64), writes=['wout'])
            S.dma('pool', BD[:], self.bdw[i].rearrange("g k m -> k g m"), writes=['BD'])
            S.dma('pool', mlo[:], self.masks[0], writes=['masks'])
            S.dma('pool', mhi[:], self.masks[1], writes=['masks'])
            S.dma('sp', evec[:], self.evec[i], writes=['evec'])
            S.dma('sp', esk[:], self.sinkrep[i:i + 1, :], writes=['esk'])
            S.op('act', lambda: nc.scalar.activation(out=eskb[:], in_=esk[:], func=AF.Exp), reads=['esk'], writes=['eskb'])
            S.op('act', lambda: nc.scalar.activation(out=csT[:], in_=evec[:, :, 9:11], func=AF.Exp, scale=-1.0), reads=['evec'], writes=['csT'])
            S.op('act', lambda: nc.scalar.activation(out=csT[:], in_=csT[:], func=AF.Ln, bias=self.one_t[:]), reads=['csT', 'one_t'], writes=['csT'])
            S.op('dve', lambda: nc.vector.tensor_scalar(out=csT[:], in0=csT[:], scalar1=-8.0, scalar2=None, op0=ALU.mult), reads=['csT'], writes=['csT'])
            S.op('pool', lambda: nc.gpsimd.memset(zt[:], 0.0), writes=['zt'])
            S.op('pool', lambda: nc.gpsimd.memset(stf[:], 0.0), writes=[('stf', c) for c in range(4)])
            S.op('pool', lambda: nc.gpsimd.memset(stb[:], 0.0), writes=[('stb', c) for c in range(4)])
            for c in range(4):
                rows = self.XAd[c * 128:(c + 1) * 128, :]
                S.dma('sp', rows[:, 0:2], zt[:, 0:2], reads=['zt'], writes=[('XApad', c)])
                S.dma('sp', rows[:, 2 + CTX:5 + CTX], zt[:, 0:3], reads=['zt'], writes=[('XApad', c)])
                S.dma('sp', rows[:, 5 + ntok:6 + ntok], zt[:, 0:1], reads=['zt'], writes=[('XApad', c)], allow_slow_non_contiguous=True)
            self.epi_load_cb(l, 1)

            with ExitStack() as p0:
                xa_sb = self.sb("xa_sb", [128, 4, 128], F32, p0)
                gg_sb = self.sb("gg_sb", [128, 4, 128], F32, p0)
                g1 = self.sb("g1", [128, 512], F32, p0)
                g2 = self.sb("g2", [128, 512], F32, p0)
                kr = self.sb("kr", [128, 128], BF16, p0)
                for gt in range(NT):
                    ui = self.rut.next()
                    self.prep_tile(l, 1, gt, self.uTt[ui], ('uTt', ui))
                    ri = self.load_rope(gt, self.ropeAw, self.ropeBw, 64)
                    for kc in range(8):
                        S.op('pe', lambda: nc.tensor.matmul(ps[:, 4, :], lhsT=self.uTt[ui][:, kc, :], rhs=win[:, kc, 1024:1536],
                                                            start=(kc == 0), stop=(kc == 7)), reads=[('uTt', ui), 'win'], writes=[('ps', 4)])
                    for kc in range(8):
                        S.op('pe', lambda: nc.tensor.matmul(ps[:, 5, 0:256], lhsT=self.uTt[ui][:, kc, :], rhs=win[:, kc, 1536:1792],
                                                            start=(kc == 0), stop=(kc == 7)), reads=[('uTt', ui), 'win'], writes=[('ps', 5)])
                    for c8 in range(8):
                        bk = c8 // 4
                        cc = c8 % 4
                        for kc in range(8):
                            S.op('pe', lambda: nc.tensor.matmul(ps[:, bk, cc * 128:(cc + 1) * 128], lhsT=win[:, kc, c8 * 128:(c8 + 1) * 128],
                                                                rhs=self.uTt[ui][:, kc, :], start=(kc == 0), stop=(kc == 7)),
                                 reads=[('uTt', ui), 'win'], writes=[('ps', bk)])
                    S.op('act', lambda: nc.scalar.copy(out=xa_sb[:].rearrange("p c t -> p (c t)"), in_=ps[:, 0, :]),
                         reads=[('ps', 0)], writes=['xa_sb'])
                    c0 = col_of(gt * 128)
                    S.dma('sp', self.XAd[:, c0:c0 + 128].rearrange("(c p) t -> p c t", p=128), xa_sb[:], reads=['xa_sb'], writes=[('XAd', gt)])
                    S.op('act', lambda: nc.scalar.activation(out=g1[:], in_=ps[:, 1, :], func=AF.Square), reads=[('ps', 1)], writes=['g1'])
                    S.op('dve', lambda: nc.vector.tensor_scalar(out=g1[:], in0=g1[:], scalar1=0.044715, scalar2=1.0, op0=ALU.mult, op1=ALU.add),
                         reads=['g1'], writes=['g1'])
                    S.op('dve', lambda: nc.vector.tensor_tensor(out=g1[:], in0=g1[:], in1=ps[:, 1, :], op=ALU.mult), reads=['g1', ('ps', 1)], writes=['g1'])
                    S.op('act', lambda: nc.scalar.activation(out=g2[:], in_=g1[:], func=AF.Sigmoid, scale=1.5957691216057308), reads=['g1'], writes=['g2'])
                    S.op('dve', lambda: nc.vector.tensor_tensor(out=gg_sb[:].rearrange("p c t -> p (c t)"), in0=g2[:], in1=ps[:, 1, :], op=ALU.mult),
                         reads=['g2', ('ps', 1)], writes=['gg_sb'])
                    S.dma('sp', self.GGd[:, gt * 128:(gt + 1) * 128].rearrange("(c p) t -> p c t", p=128), gg_sb[:], reads=['gg_sb'], writes=[('GGd', gt)])
                    S.op('act', lambda: nc.scalar.copy(out=Vw[:, gt, :], in_=ps[:, 5, 128:256]), reads=[('ps', 5)], writes=[('Vw', gt)])
                    self.normrope(ps[:, 4, :], [('ps', 4)], 8, 64, None, self.ropeA[ri][:, 0:64], self.ropeB[ri][:, 0:64], ('rope', ri),
                                  self.qr[:, 0:512], 'qr', False)
                    S.dma('sp', self.Qd[gt * 128:(gt + 1) * 128, :], self.qr[:, 0:512], reads=['qr'], writes=[('Qd', gt)])
                    self.normrope(ps[:, 5, 0:128], [('ps', 5)], 2, 64, None, self.ropeA[ri][:, 0:64], self.ropeB[ri][:, 0:64], ('rope', ri),
                                  kr[:, :], 'kr', False)
                    tb_ = 6 + self.rtp.next()
                    pT = ps[:, tb_, :].bitcast(BF16)
                    for h in range(2):
                        S.op('pe', lambda: nc.tensor.transpose(out=pT[0:64, h * 128:(h + 1) * 128], in_=kr[:, h * 64:(h + 1) * 64],
                                                               identity=self.ident[:]), reads=['kr', 'ident'], writes=[('ps', tb_)])
                    S.op('act', lambda: nc.scalar.copy(out=KTw[:, :, gt * 128:(gt + 1) * 128],
                                                       in_=pT[0:64, 0:256].rearrange("p (h t) -> p h t", h=2)),
                         reads=[('ps', tb_)], writes=[('KTw', gt)])

            S.barrier()
            tblocks = [(0, NC_)] + [(t, 4) for t in range(NC_, NT, 4)]
            with ExitStack() as p1:
                xh = [self.sb(f"xh{k}", [128, 515], F32, p1) for k in range(2)]
                xc = [self.sb(f"xc{k}", [128, 512], F32, p1) for k in range(2)]
                xcb = [self.sb(f"xcb{k}", [128, 512], BF16, p1) for k in range(2)]
                rgs = [self.sb(f"rg{k}", [128, 512], F32, p1) for k in range(2)]
                igs = [self.sb(f"ig{k}", [128, 512], F32, p1) for k in range(2)]
                av = [self.sb(f"av{k}", [128, 512], F32, p1) for k in range(2)]
                bv = [self.sb(f"bv{k}", [128, 512], F32, p1) for k in range(2)]
                a2s = [self.sb(f"a2{k}", [128, 512], F32, p1) for k in range(2)]
                rrg = Rot(2)
                hf = [self.sb(f"hf{k}", [128, 512], F32, p1) for k in range(2)]
                r2 = Rot(2)
                rab = Rot(2)
                rhf = Rot(2)
                rgp = Rot(2)
                for (t0, nt) in tblocks:
                    n = nt * 128
                    n0 = t0 * 128
                    c0 = col_of(n0)
                    for c in range(4):
                        k = r2.next()
                        S.dma('sp', xh[k][:, 0:n + 3], self.XAd[c * 128:(c + 1) * 128, c0 - 2:c0 + n + 1],
                              reads=[('XAd', t) for t in range(max(0, t0 - 1), min(NT, t0 + nt + 1))] + [('XApad', c)], writes=[('xh', k)])
                        S.op('dve', lambda: nc.vector.tensor_scalar(out=xc[k][:, 0:n], in0=xh[k][:, 0:n], scalar1=evec[:, c, 0:1],
                                                                     scalar2=evec[:, c, 4:5], op0=ALU.mult, op1=ALU.add),
                             reads=[('xh', k), 'evec'], writes=[('xc', k)])
                        for j in range(1, 4):
                            S.op('dve', lambda: nc.vector.scalar_tensor_tensor(out=xc[k][:, 0:n], in0=xh[k][:, j:j + n], scalar=evec[:, c, j:j + 1],
                                                                               in1=xc[k][:, 0:n], op0=ALU.mult, op1=ALU.add),
                                 reads=[('xh', k), 'evec', ('xc', k)], writes=[('xc', k)])
                        S.op('act', lambda: nc.scalar.copy(out=xcb[k][:, 0:n], in_=xc[k][:, 0:n]), reads=[('xc', k)], writes=[('xcb', k)])
                        for d in range(2):
                            gp = 2 * rgp.next()
                            gi_ = rrg.next()
                            rg, ig, a2 = rgs[gi_], igs[gi_], a2s[gi_]
                            S.op('pe', lambda: nc.tensor.matmul(ps[:, gp, 0:n], lhsT=BD[:, (d * 2 + 0) * 4 + c, :], rhs=xcb[k][:, 0:n],
                                                                start=True, stop=True), reads=['BD', ('xcb', k)], writes=[('ps', gp)])
                            S.op('pe', lambda: nc.tensor.matmul(ps[:, gp + 1, 0:n], lhsT=BD[:, (d * 2 + 1) * 4 + c, :], rhs=xcb[k][:, 0:n],
                                                                start=True, stop=True), reads=['BD', ('xcb', k)], writes=[('ps', gp + 1)])
                            S.op('act', lambda: nc.scalar.activation(out=rg[:, 0:n], in_=ps[:, gp, 0:n], func=AF.Sigmoid, bias=evec[:, c, 5 + d:6 + d]),
                                 reads=[('ps', gp), 'evec'], writes=[('rg', gi_)])
                            S.op('act', lambda: nc.scalar.activation(out=ig[:, 0:n], in_=ps[:, gp + 1, 0:n], func=AF.Sigmoid, bias=evec[:, c, 7 + d:8 + d]),
                                 reads=[('ps', gp + 1), 'evec'], writes=[('ig', gi_)])
                            ai = rab.next()
                            S.op('act', lambda: nc.scalar.activation(out=av[ai][:, 0:n], in_=rg[:, 0:n], func=AF.Exp, scale=csT[:, c, d:d + 1]),
                                 reads=[('rg', gi_), 'csT'], writes=[('av', ai)])
                            S.op('pool', lambda: nc.gpsimd.tensor_tensor(out=a2[:, 0:n], in0=av[ai][:, 0:n], in1=av[ai][:, 0:n], op=ALU.mult),
                                 reads=[('av', ai)], writes=[('a2', gi_)])
                            S.op('act', lambda: nc.scalar.activation(out=a2[:, 0:n], in_=a2[:, 0:n], func=AF.Sqrt, scale=-1.0, bias=self.one_t[:]),
                                 reads=[('a2', gi_), 'one_t'], writes=[('a2', gi_)])
                            S.op('pool', lambda: nc.gpsimd.tensor_tensor(out=bv[ai][:, 0:n], in0=a2[:, 0:n], in1=ig[:, 0:n], op=ALU.mult),
                                 reads=[('a2', gi_), ('ig', gi_)], writes=[('bv', ai)])
                            S.op('pool', lambda: nc.gpsimd.tensor_tensor(out=bv[ai][:, 0:n], in0=bv[ai][:, 0:n], in1=xc[k][:, 0:n], op=ALU.mult),
                                 reads=[('bv', ai), ('xc', k)], writes=[('bv', ai)])
                            if d == 0:
                                hi = rhf.next()
                                S.op('dve', lambda: nc.vector.tensor_tensor_scan(out=hf[hi][:, 0:n], data0=av[ai][:, 0:n], data1=bv[ai][:, 0:n],
                                                                                 initial=stf[:, c:c + 1], op0=ALU.mult, op1=ALU.add),
                                     reads=[('av', ai), ('bv', ai), ('stf', c)], writes=[('hf', hi)])
                                S.op('dve', lambda: nc.vector.tensor_copy(out=stf[:, c:c + 1], in_=hf[hi][:, n - 1:n]), reads=[('hf', hi)], writes=[('stf', c)])
                                S.dma('sp', self.HFd[c * 128:(c + 1) * 128, n0:n0 + n], hf[hi][:, 0:n], reads=[('hf', hi)], writes=[('HFd', t0, c)])
                            else:
                                S.dma('sp', self.ABd[c * 128:(c + 1) * 128, n0:n0 + n], av[ai][:, 0:n], reads=[('av', ai)], writes=[('ABd', t0, c)])
                                S.dma('sp', self.BBd[c * 128:(c + 1) * 128, n0:n0 + n], bv[ai][:, 0:n], reads=[('bv', ai)], writes=[('BBd', t0, c)])

            S.barrier()
            with ExitStack() as p2:
                ab = [self.sb(f"ab{k}", [128, 512], F32, p2) for k in range(1)]
                bb = [self.sb(f"bb{k}", [128, 512], F32, p2) for k in range(1)]
                hfl = [self.sb(f"hfl{k}", [128, 512], F32, p2) for k in range(1)]
                ggl = [self.sb(f"ggl{k}", [128, 512], F32, p2) for k in range(1)]
                hb = [self.sb(f"hb{k}", [128, 512], F32, p2) for k in range(1)]
                yaT = self.sb("yaT", [128, 4, 512], BF16, p2)
                obT = self.sb("obT", [64, 8, 512], BF16, p2)
                ql = [self.sb(f"ql{k}", [128, 512], BF16, p2) for k in range(2)]
                QTw = self.sb("QTw", [64, 8, 128], BF16, p2)
                pw = [self.sb(f"pw{k}", [128, 512], BF16, p2) for k in range(2)]
                rdw = self.sb("rdw", [64, 512], F32, p2)
                r2 = Rot(1)
                rq = Rot(2)
                rpw = Rot(2)
                rS = Rot(2)
                ry = Rot(2)
                order = [tblocks[0]] + list(reversed(tblocks[1:]))
                nlt = NT - NC_
                for (t0, nt) in order:
                    n = nt * 128
                    n0 = t0 * 128
                    for c in range(4):
                        k = r2.next()
                        rows = slice(c * 128, (c + 1) * 128)
                        S.dma('sp', ab[k][:, 0:n], self.ABd[rows, n0:n0 + n], reads=[('ABd', t0, c)], writes=[('ab', k)])
                        S.dma('sp', bb[k][:, 0:n], self.BBd[rows, n0:n0 + n], reads=[('BBd', t0, c)], writes=[('bb', k)])
                        S.dma('sp', hfl[k][:, 0:n], self.HFd[rows, n0:n0 + n], reads=[('HFd', t0, c)], writes=[('hfl', k)])
                        S.dma('sp', ggl[k][:, 0:n], self.GGd[rows, n0:n0 + n], reads=[('GGd', t) for t in range(t0, t0 + nt)], writes=[('ggl', k)])
                        S.op('dve', lambda: nc.vector.tensor_tensor_scan(out=hb[k][:, 0:n][:, ::-1],
                                                                         data0=ab[k][:, 0:n][:, ::-1], data1=bb[k][:, 0:n][:, ::-1],
                                                                         initial=stb[:, c:c + 1], op0=ALU.mult, op1=ALU.add),
                             reads=[('ab', k), ('bb', k), ('stb', c)], writes=[('hb', k)])
                        S.op('dve', lambda: nc.vector.tensor_copy(out=stb[:, c:c + 1], in_=hb[k][:, 0:1]), reads=[('hb', k)], writes=[('stb', c)])
                        S.op('pool', lambda: nc.gpsimd.tensor_tensor(out=hb[k][:, 0:n], in0=hb[k][:, 0:n], in1=hfl[k][:, 0:n], op=ALU.add),
                             reads=[('hb', k), ('hfl', k)], writes=[('hb', k)])
                        S.op('pool', lambda: nc.gpsimd.tensor_tensor(out=yaT[:, c, 0:n], in0=hb[k][:, 0:n], in1=ggl[k][:, 0:n], op=ALU.mult),
                             reads=[('hb', k), ('ggl', k)], writes=[('yaT', c)])
                    for tl in range(nt):
                        gt = t0 + tl
                        is_ctx = gt < NC_
                        qi = rq.next()
                        S.dma('sp', ql[qi][:], self.Qd[gt * 128:(gt + 1) * 128, :], reads=[('Qd', gt)], writes=[('ql', qi)])
                        tb_ = 6 + self.rtp.next()
                        pT = ps[:, tb_, :].bitcast(BF16)
                        for h in range(8):
                            S.op('pe', lambda: nc.tensor.transpose(out=pT[0:64, h * 128:(h + 1) * 128], in_=ql[qi][:, h * 64:(h + 1) * 64],
                                                                   identity=self.ident[:]), reads=[('ql', qi), 'ident'], writes=[('ps', tb_)])
                        S.op('act', lambda: nc.scalar.copy(out=QTw[:].rearrange("p h t -> p (h t)"), in_=pT[0:64, :]), reads=[('ps', tb_)], writes=['QTw'])
                        if is_ctx:
                            chunks = [(0, None), (1, None)]
                        else:
                            lt = gt - NC_
                            chunks = []
                            if lt > 0:
                                chunks.append((gt - 1, mlo))
                            chunks.append((gt, None))
                            if lt < nlt - 1:
                                chunks.append((gt + 1, mhi))
                            chunks += [(0, None), (1, None)]
                        for kv in range(2):
                            rhs_q = QTw[:, 4 * kv:4 * kv + 4, :].rearrange("p h t -> p (h t)")
                            for ci, (c, mask) in enumerate(chunks):
                                sbk = rS.next()
                                S.op('pe', lambda: nc.tensor.matmul(ps[:, sbk, :], lhsT=KTw[:, kv, c * 128:(c + 1) * 128], rhs=rhs_q,
                                                                    start=True, stop=(mask is None)), reads=[('KTw', c), 'QTw'], writes=[('ps', sbk)])
                                if mask is not None:
                                    S.op('pe', lambda: nc.tensor.matmul(ps[:, sbk, :], lhsT=self.ident[:], rhs=mask[:], start=False, stop=True),
                                         reads=['ident', 'masks'], writes=[('ps', sbk)])
                                pi = rpw.next()
                                S.op('act', lambda: nc.scalar.activation(out=pw[pi][:], in_=ps[:, sbk, :], func=AF.Exp, scale=0.125),
                                     reads=[('ps', sbk)], writes=[('pw', pi)])
                                S.op('pe', lambda: nc.tensor.matmul(ps[0:64, 2, :], lhsT=Vw[:, c, kv * 64:(kv + 1) * 64], rhs=pw[pi][:],
                                                                    start=(ci == 0), stop=(ci == len(chunks) - 1)),
                                     reads=[('Vw', c), ('pw', pi)], writes=[('ps', 2)])
                                S.op('pe', lambda: nc.tensor.matmul(ps[0:64, 3, :], lhsT=self.ones_bf[:, 0:64], rhs=pw[pi][:],
                                                                    start=(ci == 0), stop=False), reads=['ones_bf', ('pw', pi)], writes=[('ps', 3)])
                            S.op('pe', lambda: nc.tensor.matmul(ps[0:64, 3, :], lhsT=self.ones_bf[0:1, 0:64], rhs=eskb[0:1, kv * 512:(kv + 1) * 512],
                                                                start=False, stop=True), reads=['ones_bf', 'eskb'], writes=[('ps', 3)])
                            S.op('dve', lambda: nc.vector.reciprocal(out=rdw[:], in_=ps[0:64, 3, :]), reads=[('ps', 3)], writes=['rdw'])
                            S.op('dve', lambda: nc.vector.tensor_tensor(out=obT[:, 4 * kv:4 * kv + 4, tl * 128:(tl + 1) * 128],
                                                                        in0=ps[0:64, 2, :].rearrange("p (h t) -> p h t", h=4),
                                                                        in1=rdw[:].rearrange("p (h t) -> p h t", h=4), op=ALU.mult),
                                 reads=[('ps', 2), 'rdw'], writes=[('obT', tl)])
                    for tl in range(nt):
                        yb = 4 + 0 * ry.next()
                        for dh in range(2):
                            for c in range(4):
                                S.op('pe', lambda: nc.tensor.matmul(ps[:, yb + dh, :], lhsT=yaT[:, c, tl * 128:(tl + 1) * 128],
                                                                    rhs=woutA[:, c, dh * 512:(dh + 1) * 512], start=(c == 0), stop=False),
                                     reads=[('yaT', c), 'wout'], writes=[('ps', yb + dh)])
                            for h in range(8):
                                S.op('pe', lambda: nc.tensor.matmul(ps[:, yb + dh, :], lhsT=obT[:, h, tl * 128:(tl + 1) * 128],
                                                                    rhs=woutB[:, h, dh * 512:(dh + 1) * 512], start=False, stop=(h == 7)),
                                     reads=[('obT', tl), 'wout'], writes=[('ps', yb + dh)])
                        self.epilogue(t0 + tl, yb)

def _fm(v):
    v = np.asarray(v, np.float32)
    lead = v.shape[:-1]
    return np.ascontiguousarray(np.moveaxis(v.reshape(lead + (8, 128)), -1, 0))


def rope_tables(seq, hd):
    T = seq
    pos = np.arange(T)
    row = (pos // 64).astype(np.float32)
    col = (pos % 64).astype(np.float32)
    nf = hd // 4
    freq = (np.float32(10000.0) ** (-np.arange(nf, dtype=np.float32) / np.float32(nf))).astype(np.float32)
    ang = np.concatenate([row[:, None] * freq, col[:, None] * freq], axis=-1).astype(np.float32)
    cos, sin = np.cos(ang).astype(np.float32), np.sin(ang).astype(np.float32)
    A = np.repeat(cos, 2, axis=1)
    B = np.stack([-sin, sin], axis=-1).reshape(T, hd)
    A = np.concatenate([np.ones((CTX, hd), np.float32), A], axis=0)
    B = np.concatenate([np.zeros((CTX, hd), np.float32), B], axis=0)
    return np.ascontiguousarray(A, np.float32), np.ascontiguousarray(B, np.float32)


def make_in_maps(inp, seq=SEQ):
    maps = []
    rAg, rBg = rope_tables(seq, 128)
    rAw, rBw = rope_tables(seq, 64)
    bdw = np.zeros((2, 16, 128, 128), np.float32)
    for i in range(2):
        for d in range(2):
            for gi, key in enumerate(('lru_w_a', 'lru_w_x')):
                for c in range(4):
                    for hb in range(2):
                        bdw[i, (d * 2 + gi) * 4 + c, hb * 64:(hb + 1) * 64, hb * 64:(hb + 1) * 64] = inp[key][i, d, 2 * c + hb]
    fm4 = lambda v: np.moveaxis(np.asarray(v, np.float32).reshape(v.shape[:-1] + (4, 128)), -1, 0)
    evec = np.zeros((2, 128, 4, 11), np.float32)
    for i in range(2):
        evec[i, :, :, 0:4] = np.moveaxis(fm4(inp['even_conv_w'][i]), 1, 2)
        evec[i, :, :, 4] = fm4(inp['even_conv_b'][i])
        evec[i, :, :, 5:7] = np.moveaxis(fm4(inp['lru_b_a'][i]), 1, 2)
        evec[i, :, :, 7:9] = np.moveaxis(fm4(inp['lru_b_x'][i]), 1, 2)
        evec[i, :, :, 9:11] = np.moveaxis(fm4(inp['lru_lambda'][i]), 1, 2)
    sinkrep = np.ascontiguousarray(np.repeat(np.asarray(inp['attn_sink'], np.float32), 128, axis=1))
    ii = np.arange(128)[:, None]; jj = np.arange(128)[None, :]
    NEG = np.float32(-30000.0)
    mlo = np.where(ii >= jj, np.float32(0), NEG).astype(np.float32)
    mhi = np.where(ii <= jj, np.float32(0), NEG).astype(np.float32)
    masks = np.ascontiguousarray(np.stack([np.tile(mlo, (1, 4)), np.tile(mhi, (1, 4))], axis=0))
    w_ada = np.ascontiguousarray(inp['w_ada'], np.float32)
    b_ada = np.ascontiguousarray(inp['b_ada'], np.float32)
    b_adaT = np.ascontiguousarray(np.moveaxis(b_ada.reshape(DEPTH, 72, 128), -1, 0))
    npreT = _fm(inp['norm_pre'])
    npost = np.ascontiguousarray(inp['norm_post'], np.float32)
    for b in range(NB):
        cc = np.stack([_fm(inp['c'][b]), _fm(inp['c_ctx'])], axis=-1)
        maps.append(dict(
            x=np.ascontiguousarray(inp['x'][b, :seq], np.float32),
            ctx=np.ascontiguousarray(inp['ctx'][b], np.float32),
            ccT=np.ascontiguousarray(cc, np.float32),
            w_ada=w_ada, b_adaT=b_adaT, b_ada=b_ada, npreT=npreT, npost=npost,
            wg=np.ascontiguousarray(inp['ffn_w_gate'], np.float32),
            wu=np.ascontiguousarray(inp['ffn_w_up'], np.float32),
            wd=np.ascontiguousarray(inp['ffn_w_down'], np.float32),
            owin=np.ascontiguousarray(inp['odd_w_in'], np.float32), owout=np.ascontiguousarray(inp['odd_w_out'], np.float32),
            oqn=np.ascontiguousarray(inp['odd_q_norm'], np.float32), okn=np.ascontiguousarray(inp['odd_k_norm'], np.float32),
            ropeAg=rAg, ropeBg=rBg, ropeAw=rAw, ropeBw=rBw,
            ewin=np.ascontiguousarray(inp['even_w_in'], np.float32), ewout=np.ascontiguousarray(inp['even_w_out'], np.float32),
            bdw=bdw, evec=evec, sinkrep=sinkrep, masks=masks,
        ))
    return maps


def run(inp, seq=SEQ, stop_after=None, trace=False, force_odd=False):
    kb = K(seq=seq, stop_after=stop_after, force_odd=force_odd)
    nc = kb.build()
    maps = make_in_maps(inp, seq)
    res = run_bass_kernel_spmd(nc, maps, core_ids=list(range(NB)), trace=trace)
    out = np.stack([np.asarray(r["out"]) for r in res.results], axis=0)
    return out, res, kb


def kernel(**inputs):
    out, _, _ = run(inputs)
    return out.astype(np.float32)
```

```python
import numpy as np
from contextlib import ExitStack
import concourse.bass as bass
import concourse.mybir as mybir
from concourse.bass_utils import run_bass_kernel_spmd

F32 = mybir.dt.float32
BF16 = mybir.dt.bfloat16
ALU = mybir.AluOpType
AF = mybir.ActivationFunctionType
AX = mybir.AxisListType

D = 1024
DFF = 2816
NJ = DFF // 128
DEPTH = 4
CTX = 256
SEQ = 8192
NB = 4
EPS = 1e-6


class Sched:
    EPOCH = 16000

    def __init__(self, nc, ctx):
        self.nc = nc
        self.ctx = ctx
        self.eng = {'pe': nc.tensor, 'act': nc.scalar, 'dve': nc.vector, 'pool': nc.gpsimd, 'sp': nc.sync}
        self.esem = {}
        self.ecount = {}
        self.sem_id = 0
        for e in self.eng:
            self._new_epoch(e)
        self.water = {}
        self.lastw = {}
        self.readers = {}
        self.dma_slots = {}
        self.dma_rr = {}
        self.n_instr = 0
        self.n_wait = 0

    def _new_sem(self, name):
        self.sem_id += 1
        return self.ctx.enter_context(self.nc.semaphore(f"{name}_{self.sem_id}"))

    def _new_epoch(self, e):
        self.esem[e] = self._new_sem("t" + e)
        self.ecount[e] = 0

    def _wait(self, e, tok):
        sem, val, src = tok
        if src == 'pe' and e == 'pe':
            return
        k = (e, id(sem))
        if self.water.get(k, 0) >= val:
            return
        self.water[k] = val
        self.eng[e].wait_ge(sem, val)
        self.n_wait += 1

    def _deps(self, e, reads, writes):
        toks = []
        for r in reads:
            t = self.lastw.get(r)
            if t is not None:
                toks.append(t)
        for w in writes:
            t = self.lastw.get(w)
            if t is not None:
                toks.append(t)
            toks.extend(self.readers.get(w, ()))
        best = {}
        for t in toks:
            k = id(t[0])
            if k not in best or best[k][1] < t[1]:
                best[k] = t
        for t in best.values():
            self._wait(e, t)

    def _record(self, tok, reads, writes):
        for r in reads:
            lst = self.readers.setdefault(r, [])
            for i, t in enumerate(lst):
                if t[0] is tok[0]:
                    lst[i] = tok
                    break
            else:
                lst.append(tok)
        for w in writes:
            self.lastw[w] = tok
            self.readers[w] = []

    def op(self, e, fn, reads=(), writes=()):
        self._deps(e, reads, writes)
        if self.ecount[e] >= self.EPOCH:
            self._new_epoch(e)
        ins = fn()
        self.ecount[e] += 1
        ins.then_inc(self.esem[e], 1)
        tok = (self.esem[e], self.ecount[e], e)
        self._record(tok, reads, writes)
        self.n_instr += 1
        return tok

    def dma(self, q, out, in_, reads=(), writes=(), nslots=12, **kw):
        self._deps(q, reads, writes)
        slots = self.dma_slots.setdefault(q, [])
        if len(slots) < nslots:
            slots.append([self._new_sem("d" + q), 0])
            i = len(slots) - 1
        else:
            i = self.dma_rr.get(q, 0) % nslots
            self.dma_rr[q] = i + 1
            self._wait(q, (slots[i][0], slots[i][1], 'dma'))
        slots[i][1] += 16
        self.eng[q].dma_start(out=out, in_=in_, **kw).then_inc(slots[i][0], 16)
        tok = (slots[i][0], slots[i][1], 'dma')
        self._record(tok, reads, writes)
        self.n_instr += 1
        return tok

    def all_tokens(self):
        toks = [(self.esem[e], self.ecount[e], e) for e in self.eng if self.ecount[e] > 0]
        for q, slots in self.dma_slots.items():
            for s in slots:
                toks.append((s[0], s[1], 'dma'))
        return toks

    def barrier(self):
        toks = self.all_tokens()
        for e in self.eng:
            for t in toks:
                self._wait(e, t)
        self.lastw.clear()
        self.readers.clear()


class Rot:
    def __init__(self, n):
        self.n = n
        self.i = -1

    def next(self):
        self.i += 1
        return self.i % self.n


class K:
    def __init__(self, seq=SEQ, stop_after=None, force_odd=False):
        self.force_odd = force_odd
        self.seq = seq
        self.ntok = CTX + seq
        self.ntile = self.ntok // 128
        self.stop_after = stop_after

    def sb(self, name, shape, dt, ctx=None):
        self._uid = getattr(self, '_uid', 0) + 1
        return (ctx or self.ctx).enter_context(self.nc.sbuf_tensor(f"{name}_{self._uid}", shape, dt))

    def stream_of(self, gt):
        return 1 if gt < CTX // 128 else 0

    def blocks(self, size_tiles, with_ctx=True):
        out = []
        if with_ctx:
            out.append((0, CTX // 128))
        t = CTX // 128
        while t < self.ntile:
            n = min(size_tiles, self.ntile - t)
            out.append((t, n))
            t += n
        return out

    def build(self):
        nc = bass.Bass("TRN2", target_bir_lowering=False)
        self.nc = nc
        seq, ntok = self.seq, self.ntok
        dt_in = lambda name, shape: nc.dram_tensor(name, shape, F32, kind="ExternalInput").ap()
        self.x_in = dt_in("x", [seq, D])
        self.ctx_in = dt_in("ctx", [CTX, D])
        self.ccT = dt_in("ccT", [128, 8, 2])
        self.w_ada = dt_in("w_ada", [DEPTH, D, 9 * D])
        self.b_adaT = dt_in("b_adaT", [128, DEPTH, 72])
        self.b_ada = dt_in("b_ada", [DEPTH, 9 * D])
        self.npreT = dt_in("npreT", [128, DEPTH, 3, 8])
        self.npost = dt_in("npost", [DEPTH, 3, D])
        self.wg = dt_in("wg", [DEPTH, 2, D, DFF])
        self.wu = dt_in("wu", [DEPTH, 2, D, DFF])
        self.wd = dt_in("wd", [DEPTH, 2, DFF, D])
        self.owin = dt_in("owin", [2, D, 1536])
        self.owout = dt_in("owout", [2, D, D])
        self.oqn = dt_in("oqn", [2, 128])
        self.okn = dt_in("okn", [2, 128])
        self.ropeAg = dt_in("ropeAg", [ntok, 128])
        self.ropeBg = dt_in("ropeBg", [ntok, 128])
        self.ropeAw = dt_in("ropeAw", [ntok, 64])
        self.ropeBw = dt_in("ropeBw", [ntok, 64])
        self.ewin = dt_in("ewin", [2, D, 1792])
        self.ewout = dt_in("ewout", [2, D, D])
        self.bdw = dt_in("bdw", [2, 16, 128, 128])
        self.evec = dt_in("evec", [2, 128, 4, 11])
        self.sinkrep = dt_in("sinkrep", [2, 1024])
        self.masks = dt_in("masks", [2, 128, 512])
        self.XAd = nc.dram_tensor("XAd", [512, ntok + 6], F32, kind="Internal").ap()
        self.GGd = nc.dram_tensor("GGd", [512, ntok], F32, kind="Internal").ap()
        self.HFd = nc.dram_tensor("HFd", [512, ntok], F32, kind="Internal").ap()
        self.ABd = nc.dram_tensor("ABd", [512, ntok], F32, kind="Internal").ap()
        self.BBd = nc.dram_tensor("BBd", [512, ntok], F32, kind="Internal").ap()
        self.Qd = nc.dram_tensor("Qd", [ntok, 512], BF16, kind="Internal").ap()
        self.out = nc.dram_tensor("out", [seq, D], F32, kind="ExternalOutput").ap()
        self.X = nc.dram_tensor("Xs", [ntok, D], F32, kind="Internal").ap()
        self.Cd = nc.dram_tensor("Cd", [DEPTH, 3, 2, D], F32, kind="Internal").ap()

        with ExitStack() as ctx:
            self.ctx = ctx
            self.S = S = Sched(nc, ctx)
            self.ps = ctx.enter_context(nc.psum_tensor("ps", [128, 8, 512], F32))
            self.modT = self.sb("modT", [128, DEPTH, 72, 2], F32)
            self.AT = self.sb("AT", [128, DEPTH, 3, 8, 2], F32)
            self.identf = self.sb("identf", [128, 128], F32)
            self.ident = self.sb("ident", [128, 128], BF16)
            self.ones_bf = self.sb("ones_bf", [128, 128], BF16)
            self.eps_t = self.sb("eps_t", [128, 1], F32)
            self.one_t = self.sb("one_t", [128, 1], F32)
            self.junk = self.sb("junk", [128, 1024], BF16)
            ctx.enter_context(nc.Block())
            S.op('pool', lambda: nc.gpsimd.memset(self.identf[:], 1.0), writes=['identf'])
            S.op('pool', lambda: nc.gpsimd.affine_select(out=self.identf[:], in_=self.identf[:], pattern=[[-1, 128]],
                                                          compare_op=ALU.is_equal, fill=0.0, base=0, channel_multiplier=1),
                 reads=['identf'], writes=['identf'])
            S.op('dve', lambda: nc.vector.tensor_copy(out=self.ident[:], in_=self.identf[:]), reads=['identf'], writes=['ident'])
            S.op('pool', lambda: nc.gpsimd.memset(self.ones_bf[:], 1.0), writes=['ones_bf'])
            S.op('pool', lambda: nc.gpsimd.memset(self.eps_t[:], EPS), writes=['eps_t'])
            S.op('pool', lambda: nc.gpsimd.memset(self.one_t[:], 1.0), writes=['one_t'])

            S.dma('sp', self.X[0:CTX, :], self.ctx_in[:, :], writes=[('X', t) for t in range(CTX // 128)])
            step = 1024
            for r0 in range(0, seq, step):
                S.dma('sp', self.X[CTX + r0:CTX + r0 + step, :], self.x_in[r0:r0 + step, :],
                      writes=[('X', (CTX + r0) // 128 + t) for t in range(step // 128)])

            self.prologue()
            S.barrier()
            nsub = 0
            done = False
            for l in range(DEPTH):
                last = (l == DEPTH - 1)
                for sub in range(3):
                    if self.stop_after is not None and nsub >= self.stop_after:
                        done = True
                        break
                    final = last and sub == 2
                    if sub in (0, 2):
                        self.ffn(l, sub, with_ctx=not (last and sub == 2), final=final)
                    else:
                        self.mixer(l)
                    S.barrier()
                    nsub += 1
                if done:
                    break
            if done or True:
                pass
            if self.stop_after is not None:
                toks = []
                for r0 in range(0, seq, 1024):
                    toks.append(S.dma('sp', self.out[r0:r0 + 1024, :], self.X[CTX + r0:CTX + r0 + 1024, :],
                                      reads=[('X', (CTX + r0) // 128 + t) for t in range(8)], writes=[('out', r0)]))
            for t in S.all_tokens():
                S._wait('sp', t)
            self.n_instr = S.n_instr
            self.n_wait = S.n_wait
        return nc

    def prologue(self):
        nc, S, ps = self.nc, self.S, self.ps
        with ExitStack() as pc:
            scT = self.sb("scT", [128, 8, 2], F32, pc)
            badaT = self.sb("badaT", [128, DEPTH, 72], F32, pc)
            npreT = self.sb("npreT_s", [128, DEPTH, 3, 8], F32, pc)
            slab = [self.sb(f"adaslab{i}", [128, 8, 1024], F32, pc) for i in range(2)]
            brow = [self.sb(f"brow{i}", [2, 1024], F32, pc) for i in range(2)]
            grow = [self.sb(f"grow{i}", [2, 1024], F32, pc) for i in range(2)]
            crow = [self.sb(f"crow{i}", [2, 1024], F32, pc) for i in range(2)]
            crow2 = [self.sb(f"crow2{i}", [2, 1024], F32, pc) for i in range(2)]
            S.dma('sp', scT[:], self.ccT[:, :, :], writes=['scT'])
            S.dma('sp', badaT[:], self.b_adaT[:, :, :], writes=['badaT'])
            S.dma('sp', npreT[:], self.npreT[:, :, :, :], writes=['npreT'])
            S.op('act', lambda: nc.scalar.activation(out=scT[:], in_=scT[:], func=AF.Silu), reads=['scT'], writes=['scT'])
            rr = Rot(2)
            r2 = Rot(2)
            for l in range(DEPTH):
                for m in range(9):
                    si = rr.next()
                    wv = self.w_ada[l].rearrange("(c p) n -> p c n", p=128)
                    S.dma('sp', slab[si][:, 0:4, :], wv[:, 0:4, m * 1024:(m + 1) * 1024], writes=[('slab', si)])
                    S.dma('pool', slab[si][:, 4:8, :], wv[:, 4:8, m * 1024:(m + 1) * 1024], writes=[('slabB', si)])
                    for h in range(2):
                        for kc in range(8):
                            S.op('pe', lambda: nc.tensor.matmul(ps[0:2, 1 + h, :], lhsT=scT[:, kc, :],
                                                                rhs=slab[si][:, kc, h * 512:(h + 1) * 512],
                                                                start=(kc == 0), stop=(kc == 7)),
                                 reads=[('slab', si), ('slabB', si), 'scT'], writes=[('ps', 1 + h)])
                    ri = r2.next()
                    S.dma('sp', brow[ri][:], self.b_ada[l:l + 1, m * 1024:(m + 1) * 1024].partition_broadcast(2),
                          writes=[('brow', ri)])
                    S.op('dve', lambda: nc.vector.tensor_tensor(out=crow[ri][:], in0=ps[0:2, 1:3, :].rearrange("p a b -> p (a b)"),
                                                                in1=brow[ri][:], op=ALU.add),
                         reads=[('ps', 1), ('ps', 2), ('brow', ri)], writes=[('crow', ri)])
                    for c in range(8):
                        j = m * 8 + c
                        S.op('pe', lambda: nc.tensor.transpose(out=ps[:, 0, j * 2:j * 2 + 2], in_=crow[ri][0:2, c * 128:(c + 1) * 128],
                                                               identity=self.identf[0:2, 0:2]),
                             reads=[('crow', ri), 'identf'], writes=[('ps', 0)])
                    if m in (2, 5, 8):
                        sub = m // 3
                        S.dma('sp', grow[ri][:], self.npost[l, sub:sub + 1, :].partition_broadcast(2), writes=[('grow', ri)])
                        S.op('dve', lambda: nc.vector.scalar_tensor_tensor(out=crow2[ri][:], in0=crow[ri][:],
                                                                           scalar=(1.0 if sub == 1 else 0.5), in1=grow[ri][:],
                                                                           op0=ALU.mult, op1=ALU.mult),
                             reads=[('crow', ri), ('grow', ri)], writes=[('crow2', ri)])
                        S.dma('sp', self.Cd[l, sub, :, :], crow2[ri][:], reads=[('crow2', ri)], writes=[('Cd', l, sub)])
                S.op('dve', lambda: nc.vector.tensor_copy(out=self.modT[:, l, :, :],
                                                          in_=ps[:, 0, 0:144].rearrange("p (j s) -> p j s", s=2)),
                     reads=[('ps', 0)], writes=['modT'])
                for sub in range(3):
                    m = 3 * sub + 1
                    S.op('dve', lambda: nc.vector.scalar_tensor_tensor(
                        out=self.AT[:, l, sub, :, :], in0=self.modT[:, l, m * 8:(m + 1) * 8, :], scalar=1.0,
                        in1=npreT[:, l, sub, :].unsqueeze(2).to_broadcast([128, 8, 2]), op0=ALU.add, op1=ALU.mult),
                        reads=['modT', 'npreT'], writes=['AT'])
            S.barrier()

    def prep_alloc(self, pc):
        self.xin = [self.sb(f"xin{i}", [128, 1024], F32, pc) for i in range(2)]
        self.xn = [self.sb(f"xn{i}", [128, 1024], BF16, pc) for i in range(2)]
        self.ssb = self.sb("ssb", [128, 8], F32, pc)
        self.rx = Rot(2)
        self.rn = Rot(2)
        self.rs = Rot(8)
        self.rtp = Rot(2)

    def prep_tile(self, l, sub, gt, uT_dst, uT_res, tbank=None):
        nc, S, ps = self.nc, self.S, self.ps
        s = self.stream_of(gt)
        xi = self.rx.next()
        S.dma('sp', self.xin[xi][:], self.X[gt * 128:(gt + 1) * 128, :], reads=[('X', gt)], writes=[('xin', xi)])
        si = self.rs.next()
        ssc = self.ssb[:, si:si + 1]
        S.op('act', lambda: nc.scalar.activation(out=self.junk[:], in_=self.xin[xi][:], func=AF.Square, accum_out=ssc),
             reads=[('xin', xi)], writes=[('ss', si)])
        S.op('act', lambda: nc.scalar.activation(out=ssc, in_=ssc, func=AF.Sqrt, scale=1.0 / D, bias=self.eps_t[:]),
             reads=[('ss', si), 'eps_t'], writes=[('ss', si)])
        S.op('dve', lambda: nc.vector.reciprocal(out=ssc, in_=ssc), reads=[('ss', si)], writes=[('ss', si)])
        ni = self.rn.next()
        S.op('act', lambda: nc.scalar.activation(out=self.xn[ni][:], in_=self.xin[xi][:], func=AF.Identity, scale=ssc),
             reads=[('xin', xi), ('ss', si)], writes=[('xn', ni)])
        bk = (6 + self.rtp.next()) if tbank is None else tbank
        pT = ps[:, bk, :].bitcast(BF16)
        for c in range(8):
            S.op('pe', lambda: nc.tensor.transpose(out=pT[:, c * 128:(c + 1) * 128], in_=self.xn[ni][:, c * 128:(c + 1) * 128],
                                                   identity=self.ident[:]),
                 reads=[('xn', ni), 'ident'], writes=[('ps', bk)])
        for c in range(8):
            mb = 3 * sub
            S.op('dve', lambda: nc.vector.tensor_scalar(out=uT_dst[:, c, :], in0=pT[:, c * 128:(c + 1) * 128],
                                                         scalar1=self.AT[:, l, sub, c, s:s + 1],
                                                         scalar2=self.modT[:, l, mb * 8 + c, s:s + 1],
                                                         op0=ALU.mult, op1=ALU.add),
                 reads=[('ps', bk), 'AT', 'modT'], writes=[uT_res])

    def epi_alloc(self, pc):
        self.CB = [self.sb(f"CB{i}", [128, 1024], F32, pc) for i in range(2)]
        self.xr = [self.sb(f"xr{i}", [128, 1024], F32, pc) for i in range(1)]
        self.tb = [self.sb(f"tb{i}", [128, 1024], F32, pc) for i in range(1)]
        self.ob = [self.sb(f"ob{i}", [128, 1024], F32, pc) for i in range(2)]
        self.ss2 = self.sb("ss2", [128, 8], F32, pc)
        self.rxr = Rot(1)
        self.rtb = Rot(1)
        self.rob = Rot(2)
        self.rs2 = Rot(8)

    def epi_load_cb(self, l, sub):
        for s in range(2):
            self.S.dma('sp', self.CB[s][:], self.Cd[l, sub, s:s + 1, :].partition_broadcast(128),
                       reads=[('Cd', l, sub)], writes=[('CB', s)])

    def epilogue(self, gt, ybank, final=False):
        nc, S, ps = self.nc, self.S, self.ps
        s = self.stream_of(gt)
        yp = ps[:, ybank:ybank + 2, :].rearrange("p a b -> p (a b)")
        yres = [('ps', ybank), ('ps', ybank + 1)]
        xi = self.rxr.next()
        S.dma('sp', self.xr[xi][:], self.X[gt * 128:(gt + 1) * 128, :], reads=[('X', gt)], writes=[('xr', xi)])
        si = self.rs2.next()
        ssc = self.ss2[:, si:si + 1]
        S.op('act', lambda: nc.scalar.activation(out=self.junk[:], in_=yp, func=AF.Square, accum_out=ssc),
             reads=yres, writes=[('ss2', si)])
        S.op('act', lambda: nc.scalar.activation(out=ssc, in_=ssc, func=AF.Sqrt, scale=1.0 / D, bias=self.eps_t[:]),
             reads=[('ss2', si), 'eps_t'], writes=[('ss2', si)])
        S.op('dve', lambda: nc.vector.reciprocal(out=ssc, in_=ssc), reads=[('ss2', si)], writes=[('ss2', si)])
        ti = self.rtb.next()
        S.op('dve', lambda: nc.vector.tensor_tensor(out=self.tb[ti][:], in0=yp, in1=self.CB[s][:], op=ALU.mult),
             reads=yres + [('CB', s)], writes=[('tb', ti)])
        oi = self.rob.next()
        S.op('dve', lambda: nc.vector.scalar_tensor_tensor(out=self.ob[oi][:], in0=self.tb[ti][:], scalar=ssc, in1=self.xr[xi][:],
                                                           op0=ALU.mult, op1=ALU.add),
             reads=[('tb', ti), ('ss2', si), ('xr', xi)], writes=[('ob', oi)])
        if final:
            lt = gt - CTX // 128
            S.dma('sp', self.out[lt * 128:(lt + 1) * 128, :], self.ob[oi][:], reads=[('ob', oi)], writes=[('out', lt)])
        else:
            S.dma('sp', self.X[gt * 128:(gt + 1) * 128, :], self.ob[oi][:], reads=[('ob', oi)], writes=[('X', gt)])

    def ffn(self, l, sub, with_ctx=True, final=False):
        nc, S, ps = self.nc, self.S, self.ps
        f = 0 if sub == 0 else 1
        with ExitStack() as pc:
            self.prep_alloc(pc)
            self.epi_alloc(pc)
            wd_sb = self.sb("wd_sb", [128, NJ, 1024], BF16, pc)
            uT = [self.sb(f"uT{i}", [128, 8, 1024], BF16, pc) for i in range(2)]
            actT = self.sb("actT", [128, NJ, 1024], BF16, pc)
            NSL = 3
            wgs = [self.sb(f"wgs{i}", [128, 8, 256], BF16, pc) for i in range(NSL)]
            wus = [self.sb(f"wus{i}", [128, 8, 256], BF16, pc) for i in range(NSL)]
            sg = [self.sb(f"sg{i}", [128, 512], F32, pc) for i in range(2)]
            wgv = self.wg[l, f].rearrange("(c p) n -> p c n", p=128)
            wuv = self.wu[l, f].rearrange("(c p) n -> p c n", p=128)
            wdv = self.wd[l, f].rearrange("(j p) n -> p j n", p=128)
            self.epi_load_cb(l, sub)
            for j0 in range(0, NJ, 6):
                j1 = min(NJ, j0 + 6)
                S.dma('pool', wd_sb[:, j0:j1, :], wdv[:, j0:j1, :], writes=[('wd', j) for j in range(j0, j1)])
            blocks = self.blocks(8, with_ctx)
            rsl = Rot(NSL)
            rp1 = Rot(2)
            rsg = Rot(2)
            ry = Rot(2)

            def prep_block(bi):
                t0, nt = blocks[bi]
                par = bi % 2
                return [(lambda tt=tt: self.prep_tile(l, sub, t0 + tt, uT[par][:, :, tt * 128:(tt + 1) * 128], ('uT', par, tt)))
                        for tt in range(nt)]

            for fn in prep_block(0):
                fn()
            for bi, (t0, nt) in enumerate(blocks):
                par = bi % 2
                pending = prep_block(bi + 1) if bi + 1 < len(blocks) else []
                ntk = nt * 128
                halves = [(h0, min(512, ntk - h0)) for h0 in range(0, ntk, 512)]
                for js in range(0, NJ, 2):
                    sl = rsl.next()
                    S.dma('pool', wgs[sl][:], wgv[:, :, js * 128:js * 128 + 256], writes=[('wgs', sl)])
                    S.dma('pool', wus[sl][:], wuv[:, :, js * 128:js * 128 + 256], writes=[('wus', sl)])
                    for jj in range(2):
                        j = js + jj
                        for (h0, n) in halves:
                            pp = rp1.next()
                            bg, bu = 2 * pp, 2 * pp + 1
                            ures = [('uT', par, tt) for tt in range(h0 // 128, (h0 + n) // 128)]
                            for kc in range(8):
                                S.op('pe', lambda: nc.tensor.matmul(ps[:, bg, 0:n], lhsT=wgs[sl][:, kc, jj * 128:(jj + 1) * 128],
                                                                    rhs=uT[par][:, kc, h0:h0 + n], start=(kc == 0), stop=(kc == 7)),
                                     reads=[('wgs', sl)] + ures, writes=[('ps', bg)])
                            for kc in range(8):
                                S.op('pe', lambda: nc.tensor.matmul(ps[:, bu, 0:n], lhsT=wus[sl][:, kc, jj * 128:(jj + 1) * 128],
                                                                    rhs=uT[par][:, kc, h0:h0 + n], start=(kc == 0), stop=(kc == 7)),
                                     reads=[('wus', sl)] + ures, writes=[('ps', bu)])
                            gi = rsg.next()
                            S.op('act', lambda: nc.scalar.activation(out=sg[gi][:, 0:n], in_=ps[:, bg, 0:n], func=AF.Silu),
                                 reads=[('ps', bg)], writes=[('sg', gi)])
                            S.op('dve', lambda: nc.vector.tensor_tensor(out=actT[:, j, h0:h0 + n], in0=sg[gi][:, 0:n],
                                                                        in1=ps[:, bu, 0:n], op=ALU.mult),
                                 reads=[('sg', gi), ('ps', bu)], writes=[('act', j, h0)])
                    if pending and js >= 2:
                        pending.pop(0)()
                while pending:
                    pending.pop(0)()
                for tt in range(nt):
                    yb = 4 + 2 * ry.next()
                    h0 = (tt * 128) // 512 * 512
                    for dh in range(2):
                        for j in range(NJ):
                            S.op('pe', lambda: nc.tensor.matmul(ps[:, yb + dh, :], lhsT=actT[:, j, tt * 128:(tt + 1) * 128],
                                                                rhs=wd_sb[:, j, dh * 512:(dh + 1) * 512],
                                                                start=(j == 0), stop=(j == NJ - 1)),
                                 reads=[('act', j, h0), ('wd', j)], writes=[('ps', yb + dh)])
                    self.epilogue(t0 + tt, yb, final=final)

    def mixer(self, l):
        if l % 2 == 1 or self.force_odd:
            self.odd_mixer(l)
        else:
            self.even_mixer(l)

    def normrope(self, src_ps, src_res, H, hd, gain, ra, rb, rope_res, dst_bf, dst_res, do_norm):
        nc, S = self.nc, self.S
        W = H * hd
        t = self.nrt[:, 0:W]
        v3 = lambda ap: ap.rearrange("p (h d) -> p h d", h=H)
        v4 = lambda ap: ap.rearrange("p (h i two) -> p h i two", h=H, two=2)
        if do_norm:
            sq = self.tb[0][:, 0:W]
            S.op('act', lambda: nc.scalar.activation(out=sq, in_=src_ps, func=AF.Square), reads=src_res, writes=[('tb', 0)])
            k0 = 8 * self.rsh.next()
            ssh = self.ssh[:, k0:k0 + H]
            S.op('dve', lambda: nc.vector.tensor_reduce(out=ssh, in_=v3(sq), axis=AX.X, op=ALU.add),
                 reads=[('tb', 0)], writes=[('ssh', k0)])
            S.op('act', lambda: nc.scalar.activation(out=ssh, in_=ssh, func=AF.Sqrt, scale=1.0 / hd, bias=self.eps_t[:]),
                 reads=[('ssh', k0), 'eps_t'], writes=[('ssh', k0)])
            S.op('dve', lambda: nc.vector.reciprocal(out=ssh, in_=ssh), reads=[('ssh', k0)], writes=[('ssh', k0)])
            S.op('dve', lambda: nc.vector.tensor_tensor(out=v3(t), in0=v3(src_ps), in1=ssh.unsqueeze(2).to_broadcast([128, H, hd]),
                                                        op=ALU.mult), reads=src_res + [('ssh', k0)], writes=['nrt'])
            S.op('pool', lambda: nc.gpsimd.tensor_tensor(out=v3(t), in0=v3(t), in1=gain[:].unsqueeze(1).to_broadcast([128, H, hd]),
                                                         op=ALU.mult), reads=['nrt', 'gains'], writes=['nrt'])
        else:
            S.op('act', lambda: nc.scalar.copy(out=t, in_=src_ps), reads=src_res, writes=['nrt'])
        A = self.xr[0][:, 0:W]
        B = self.ob[0][:, 0:W]
        S.op('dve', lambda: nc.vector.tensor_tensor(out=v3(A), in0=v3(t), in1=ra.unsqueeze(1).to_broadcast([128, H, hd]), op=ALU.mult),
             reads=['nrt', rope_res], writes=[('xr', 0)])
        rb3 = rb.rearrange("p (i two) -> p i two", two=2)
        S.op('pool', lambda: nc.gpsimd.tensor_tensor(out=v4(B), in0=v4(t)[:, :, :, ::-1],
                                                     in1=rb3.unsqueeze(1).to_broadcast([128, H, hd // 2, 2]), op=ALU.mult),
             reads=['nrt', rope_res], writes=[('ob', 0)])
        S.op('dve', lambda: nc.vector.tensor_tensor(out=dst_bf, in0=A, in1=B, op=ALU.add),
             reads=[('xr', 0), ('ob', 0)], writes=[dst_res])

    def load_rope(self, gt, tabA, tabB, hd):
        ri = self.rrope.next()
        self.S.dma('sp', self.ropeA[ri][:, 0:hd], tabA[gt * 128:(gt + 1) * 128, :], writes=[('rope', ri)])
        self.S.dma('sp', self.ropeB[ri][:, 0:hd], tabB[gt * 128:(gt + 1) * 128, :], writes=[('rope', ri)])
        return ri

    def mix_alloc(self, pc):
        self.prep_alloc(pc)
        self.epi_alloc(pc)
        self.nrt = self.sb("nrt", [128, 1024], F32, pc)
        self.ssh = self.sb("ssh", [128, 32], F32, pc)
        self.rsh = Rot(4)
        self.ropeA = [self.sb(f"ropeA{i}", [128, 128], F32, pc) for i in range(2)]
        self.ropeB = [self.sb(f"ropeB{i}", [128, 128], F32, pc) for i in range(2)]
        self.rrope = Rot(2)
        self.uTt = [self.sb(f"uTt{i}", [128, 8, 128], BF16, pc) for i in range(2)]
        self.rut = Rot(2)
        self.qr = self.sb("qr", [128, 1024], BF16, pc)
        self.negshift = self.sb("negshift", [128, 1], F32, pc)
        self.S.op('pool', lambda: self.nc.gpsimd.memset(self.negshift[:], -4.0), writes=['negshift'])

    def odd_mixer(self, l):
        nc, S, ps = self.nc, self.S, self.ps
        i = l // 2
        last = (l == DEPTH - 1)
        NT = self.ntile
        with ExitStack() as pc:
            self.mix_alloc(pc)
            win = self.sb("win", [128, 8, 1536], BF16, pc)
            wout = self.sb("wout", [128, 8, 1024], BF16, pc)
            KT = self.sb("KT", [128, 2, NT * 128], BF16, pc)
            V = self.sb("V", [128, NT, 2, 130], BF16, pc)
            Otok = self.sb("Otok", [128, 8, 4, 128], BF16, pc)
            rden4 = self.sb("rden4", [128, 4], F32, pc)
            QT = self.sb("QT", [128, 8, 512], BF16, pc)
            OT = self.sb("OT", [128, 8, 512], BF16, pc)
            pbuf = [self.sb(f"pbuf{i}", [128, 2, 512], BF16, pc) for i in range(2)]
            gq = self.sb("gq", [128, 128], F32, pc)
            gk = self.sb("gk", [128, 128], F32, pc)
            S.dma('pool', win[:], self.owin[i].rearrange("(c p) n -> p c n", p=128), writes=['win'])
            S.dma('pool', wout[:], self.owout[i].rearrange("(c p) n -> p c n", p=128), writes=['wout'])
            S.dma('sp', gq[:], self.oqn[i:i + 1, :].partition_broadcast(128), writes=['gains'])
            S.dma('sp', gk[:], self.okn[i:i + 1, :].partition_broadcast(128), writes=['gains'])
            self.epi_load_cb(l, 1)
            S.op('pool', lambda: nc.gpsimd.memset(V[:, :, :, 128:130], 1.0), writes=['Vones'])
            rkv = Rot(2)
            def p1A(gt):
                par = gt % 2
                ui = self.rut.next()
                self.prep_tile(l, 1, gt, self.uTt[ui], ('uTt', ui), tbank=4 + par)
                ri = self.load_rope(gt, self.ropeAg, self.ropeBg, 128)
                bk = par
                for kc in range(8):
                    S.op('pe', lambda: nc.tensor.matmul(ps[:, bk, :], lhsT=self.uTt[ui][:, kc, :], rhs=win[:, kc, 1024:1536],
                                                        start=(kc == 0), stop=(kc == 7)),
                         reads=[('uTt', ui), 'win'], writes=[('ps', bk)])
                return ri

            def p1B(gt, ri):
                bk = gt % 2
                S.op('act', lambda: nc.scalar.copy(out=V[:, gt, :, 0:128], in_=ps[:, bk, 256:512].rearrange("p (k d) -> p k d", k=2)),
                     reads=[('ps', bk)], writes=[('V', gt)])
                self.normrope(ps[:, bk, 0:256], [('ps', bk)], 2, 128, gk, self.ropeA[ri][:, 0:128], self.ropeB[ri][:, 0:128],
                              ('rope', ri), self.qr[:, 0:256], 'qr', True)

            def p1C(gt):
                tb_ = 6 + gt % 2
                pT = ps[:, tb_, :].bitcast(BF16)
                for h in range(2):
                    S.op('pe', lambda: nc.tensor.transpose(out=pT[:, h * 128:(h + 1) * 128], in_=self.qr[:, h * 128:(h + 1) * 128],
                                                           identity=self.ident[:]), reads=['qr', 'ident'], writes=[('ps', tb_)])
                S.op('act', lambda: nc.scalar.copy(out=KT[:, :, gt * 128:(gt + 1) * 128],
                                                   in_=pT[:, 0:256].rearrange("p (h t) -> p h t", h=2)),
                     reads=[('ps', tb_)], writes=[('KT', gt)])

            ris = {0: p1A(0)}
            for gt in range(NT):
                if gt + 1 < NT:
                    ris[gt + 1] = p1A(gt + 1)
                p1B(gt, ris.pop(gt))
                p1C(gt)
            qblocks = self.blocks(4, with_ctx=not last)
            rS = Rot(2)
            rO = Rot(2)
            rP = Rot(2)
            ry = Rot(2)
            sc = 128.0 ** -0.5
            for (t0, nt) in qblocks:
                nq = nt * 128
                is_ctx = t0 < CTX // 128
                def qA(tl):
                    gt = t0 + tl
                    par = tl % 2
                    ui = self.rut.next()
                    self.prep_tile(l, 1, gt, self.uTt[ui], ('uTt', ui), tbank=4 + par)
                    ri = self.load_rope(gt, self.ropeAg, self.ropeBg, 128)
                    for half in range(2):
                        for kc in range(8):
                            S.op('pe', lambda: nc.tensor.matmul(ps[:, 2 * par + half, :], lhsT=self.uTt[ui][:, kc, :],
                                                                rhs=win[:, kc, half * 512:(half + 1) * 512],
                                                                start=(kc == 0), stop=(kc == 7)),
                                 reads=[('uTt', ui), 'win'], writes=[('ps', 2 * par + half)])
                    return ri

                def qB(tl, ri):
                    par = tl % 2
                    self.normrope(ps[:, 2 * par:2 * par + 2, :].rearrange("p a b -> p (a b)"), [('ps', 2 * par), ('ps', 2 * par + 1)], 8, 128, gq,
                                  self.ropeA[ri][:, 0:128], self.ropeB[ri][:, 0:128], ('rope', ri), self.qr[:, :], 'qr', True)

                def qC(tl):
                    tb_ = 6 + tl % 2
                    pT = ps[:, tb_, :].bitcast(BF16)
                    for h in range(8):
                        S.op('pe', lambda: nc.tensor.transpose(out=pT[:, h * 128:(h + 1) * 128], in_=self.qr[:, h * 128:(h + 1) * 128],
                                                               identity=self.ident[:]), reads=['qr', 'ident'], writes=[('ps', tb_)])
                    S.op('act', lambda: nc.scalar.copy(out=QT[:, :, tl * 128:(tl + 1) * 128],
                                                       in_=pT.rearrange("p (h t) -> p h t", h=8)),
                         reads=[('ps', tb_)], writes=['QT'])

                qri = {0: qA(0)}
                for tl in range(nt):
                    if tl + 1 < nt:
                        qri[tl + 1] = qA(tl + 1)
                    qB(tl, qri.pop(tl))
                    qC(tl)
                chunks = list(range(CTX // 128)) if is_ctx else list(range(NT))
                nqt = nq // 128
                steps = [(h, c) for h in range(8) for c in chunks]
                rS4 = Rot(4)
                rP4 = Rot(4)

                def emit_S(st):
                    h, c = st
                    kv = h // 4
                    bk_ = rS4.next()
                    S.op('pe', lambda: nc.tensor.matmul(ps[:, bk_, 0:nq], lhsT=KT[:, kv, c * 128:(c + 1) * 128],
                                                        rhs=QT[:, h, 0:nq], start=True, stop=True),
                         reads=[('KT', c), 'QT'], writes=[('ps', bk_)])
                    return bk_

                def emit_rest(st, bk_):
                    h, c = st
                    kv = h // 4
                    pi = rP4.next()
                    pb = pbuf[pi // 2][:, pi % 2, :]
                    S.op('act', lambda: nc.scalar.activation(out=pb[:, 0:nq], in_=ps[:, bk_, 0:nq],
                                                             func=AF.Exp, scale=sc, bias=self.negshift[:]),
                         reads=[('ps', bk_), 'negshift'], writes=[('pbuf', pi)])
                    first = (c == chunks[0])
                    lastc = (c == chunks[-1])
                    for qt in range(nqt):
                        S.op('pe', lambda: nc.tensor.matmul(ps[:, 4 + qt, 0:130], lhsT=pb[:, qt * 128:(qt + 1) * 128],
                                                            rhs=V[:, c, kv, :], start=first, stop=lastc),
                             reads=[('V', c), 'Vones', ('pbuf', pi)], writes=[('ps', 4 + qt)])
                    if lastc:
                        ores = [('ps', 4 + qt) for qt in range(nqt)]
                        S.op('dve', lambda: nc.vector.reciprocal(out=rden4[:, 0:nqt].unsqueeze(2), in_=ps[:, 4:4 + nqt, 128:129]),
                             reads=ores, writes=['rden4'])
                        S.op('dve', lambda: nc.vector.tensor_tensor(out=Otok[:, h, 0:nqt, :], in0=ps[:, 4:4 + nqt, 0:128],
                                                                    in1=rden4[:, 0:nqt].unsqueeze(2).to_broadcast([128, nqt, 128]),
                                                                    op=ALU.mult),
                             reads=ores + ['rden4'], writes=[('Otok', h)])

                PD = 3
                pend = []
                nxt = 0
                while nxt < len(steps) or pend:
                    while nxt < len(steps) and len(pend) < PD + 1:
                        pend.append((steps[nxt], emit_S(steps[nxt])))
                        nxt += 1
                    st, bk_ = pend.pop(0)
                    emit_rest(st, bk_)
                for h in range(8):
                    bk_ = h % 4
                    pT = ps[:, bk_, :].bitcast(BF16)
                    for qt in range(nqt):
                        S.op('pe', lambda: nc.tensor.transpose(out=pT[:, qt * 128:(qt + 1) * 128], in_=Otok[:, h, qt, :],
                                                               identity=self.ident[:]), reads=[('Otok', h), 'ident'], writes=[('ps', bk_)])
                    if h % 2 == 0:
                        S.op('act', lambda: nc.scalar.copy(out=OT[:, h, 0:nq], in_=pT[:, 0:nq]), reads=[('ps', bk_)], writes=[('OT', h)])
                    else:
                        S.op('dve', lambda: nc.vector.tensor_copy(out=OT[:, h, 0:nq], in_=pT[:, 0:nq]), reads=[('ps', bk_)], writes=[('OT', h)])
                for tl in range(nt):
                    yb = 2 * ry.next()
                    for dh in range(2):
                        for h in range(8):
                            S.op('pe', lambda: nc.tensor.matmul(ps[:, yb + dh, :], lhsT=OT[:, h, tl * 128:(tl + 1) * 128],
                                                                rhs=wout[:, h, dh * 512:(dh + 1) * 512], start=(h == 0), stop=(h == 7)),
                                 reads=[('OT', h), 'wout'], writes=[('ps', yb + dh)])
                    self.epilogue(t0 + tl, yb)

    def even_mixer(self, l):
        nc, S, ps = self.nc, self.S, self.ps
        i = l // 2
        NT = self.ntile
        NC_ = CTX // 128
        seq, ntok = self.seq, self.ntok
        col_of = lambda n: (2 + n) if n < CTX else (5 + n)
        with ExitStack() as pc:
            self.mix_alloc(pc)
            win = self.sb("ewin", [128, 8, 1792], BF16, pc)
            woutA = self.sb("ewoutA", [128, 4, 1024], BF16, pc)
            woutB = self.sb("ewoutB", [64, 8, 1024], BF16, pc)
            KTw = self.sb("KTw", [64, 2, NT * 128], BF16, pc)
            Vw = self.sb("Vw", [128, NT, 128], BF16, pc)
            BD = self.sb("BD", [128, 16, 128], BF16, pc)
            evec = self.sb("evec", [128, 4, 11], F32, pc)
            csT = self.sb("csT", [128, 4, 2], F32, pc)
            mlo = self.sb("mlo", [128, 512], BF16, pc)
            mhi = self.sb("mhi", [128, 512], BF16, pc)
            esk = self.sb("esk", [1, 1024], F32, pc)
            eskb = self.sb("eskb", [1, 1024], BF16, pc)
            zt = self.sb("zt", [128, 4], F32, pc)
            stf = self.sb("stf", [128, 4], F32, pc)
            stb = self.sb("stb", [128, 4], F32, pc)
            S.dma('pool', win[:], self.ewin[i].rearrange("(c p) n -> p c n", p=128), writes=['win'])
            S.dma('pool', woutA[:], self.ewout[i, 0:512, :].rearrange("(c p) n -> p c n", p=128), writes=['wout'])
            S.dma('pool', woutB[:], self.ewout[i, 512:1024, :].rearrange("(h d) n -> d h n", d=64), writes=['wout'])
            S.dma('pool', BD[:], self.bdw[i].rearrange("g k m -> k g m"), writes=['BD'])
            S.dma('pool', mlo[:], self.masks[0], writes=['masks'])
            S.dma('pool', mhi[:], self.masks[1], writes=['masks'])
            S.dma('sp', evec[:], self.evec[i], writes=['evec'])
            S.dma('sp', esk[:], self.sinkrep[i:i + 1, :], writes=['esk'])
            S.op('act', lambda: nc.scalar.activation(out=eskb[:], in_=esk[:], func=AF.Exp), reads=['esk'], writes=['eskb'])
            S.op('act', lambda: nc.scalar.activation(out=csT[:], in_=evec[:, :, 9:11], func=AF.Exp, scale=-1.0), reads=['evec'], writes=['csT'])
            S.op('act', lambda: nc.scalar.activation(out=csT[:], in_=csT[:], func=AF.Ln, bias=self.one_t[:]), reads=['csT', 'one_t'], writes=['csT'])
            S.op('dve', lambda: nc.vector.tensor_scalar(out=csT[:], in0=csT[:], scalar1=-8.0, scalar2=None, op0=ALU.mult), reads=['csT'], writes=['csT'])
            S.op('pool', lambda: nc.gpsimd.memset(zt[:], 0.0), writes=['zt'])
            S.op('pool', lambda: nc.gpsimd.memset(stf[:], 0.0), writes=[('stf', c) for c in range(4)])
            S.op('pool', lambda: nc.gpsimd.memset(stb[:], 0.0), writes=[('stb', c) for c in range(4)])
            for c in range(4):
                rows = self.XAd[c * 128:(c + 1) * 128, :]
                S.dma('sp', rows[:, 0:2], zt[:, 0:2], reads=['zt'], writes=[('XApad', c)])
                S.dma('sp', rows[:, 2 + CTX:5 + CTX], zt[:, 0:3], reads=['zt'], writes=[('XApad', c)])
                S.dma('sp', rows[:, 5 + ntok:6 + ntok], zt[:, 0:1], reads=['zt'], writes=[('XApad', c)], allow_slow_non_contiguous=True)
            self.epi_load_cb(l, 1)

            with ExitStack() as p0:
                xa_sb = self.sb("xa_sb", [128, 4, 128], F32, p0)
                gg_sb = self.sb("gg_sb", [128, 4, 128], F32, p0)
                g1 = self.sb("g1", [128, 512], F32, p0)
                g2 = self.sb("g2", [128, 512], F32, p0)
                kr = self.sb("kr", [128, 128], BF16, p0)
                for gt in range(NT):
                    ui = self.rut.next()
                    self.prep_tile(l, 1, gt, self.uTt[ui], ('uTt', ui))
                    ri = self.load_rope(gt, self.ropeAw, self.ropeBw, 64)
                    for kc in range(8):
                        S.op('pe', lambda: nc.tensor.matmul(ps[:, 4, :], lhsT=self.uTt[ui][:, kc, :], rhs=win[:, kc, 1024:1536],
                                                            start=(kc == 0), stop=(kc == 7)), reads=[('uTt', ui), 'win'], writes=[('ps', 4)])
                    for kc in range(8):
                        S.op('pe', lambda: nc.tensor.matmul(ps[:, 5, 0:256], lhsT=self.uTt[ui][:, kc, :], rhs=win[:, kc, 1536:1792],
                                                            start=(kc == 0), stop=(kc == 7)), reads=[('uTt', ui), 'win'], writes=[('ps', 5)])
                    for c8 in range(8):
                        bk = c8 // 4
                        cc = c8 % 4
                        for kc in range(8):
                            S.op('pe', lambda: nc.tensor.matmul(ps[:, bk, cc * 128:(cc + 1) * 128], lhsT=win[:, kc, c8 * 128:(c8 + 1) * 128],
                                                                rhs=self.uTt[ui][:, kc, :], start=(kc == 0), stop=(kc == 7)),
                                 reads=[('uTt', ui), 'win'], writes=[('ps', bk)])
                    S.op('act', lambda: nc.scalar.copy(out=xa_sb[:].rearrange("p c t -> p (c t)"), in_=ps[:, 0, :]),
                         reads=[('ps', 0)], writes=['xa_sb'])
                    c0 = col_of(gt * 128)
                    S.dma('sp', self.XAd[:, c0:c0 + 128].rearrange("(c p) t -> p c t", p=128), xa_sb[:], reads=['xa_sb'], writes=[('XAd', gt)])
                    S.op('act', lambda: nc.scalar.activation(out=g1[:], in_=ps[:, 1, :], func=AF.Square), reads=[('ps', 1)], writes=['g1'])
                    S.op('dve', lambda: nc.vector.tensor_scalar(out=g1[:], in0=g1[:], scalar1=0.044715, scalar2=1.0, op0=ALU.mult, op1=ALU.add),
                         reads=['g1'], writes=['g1'])
                    S.op('dve', lambda: nc.vector.tensor_tensor(out=g1[:], in0=g1[:], in1=ps[:, 1, :], op=ALU.mult), reads=['g1', ('ps', 1)], writes=['g1'])
                    S.op('act', lambda: nc.scalar.activation(out=g2[:], in_=g1[:], func=AF.Sigmoid, scale=1.5957691216057308), reads=['g1'], writes=['g2'])
                    S.op('dve', lambda: nc.vector.tensor_tensor(out=gg_sb[:].rearrange("p c t -> p (c t)"), in0=g2[:], in1=ps[:, 1, :], op=ALU.mult),
                         reads=['g2', ('ps', 1)], writes=['gg_sb'])
                    S.dma('sp', self.GGd[:, gt * 128:(gt + 1) * 128].rearrange("(c p) t -> p c t", p=128), gg_sb[:], reads=['gg_sb'], writes=[('GGd', gt)])
                    S.op('act', lambda: nc.scalar.copy(out=Vw[:, gt, :], in_=ps[:, 5, 128:256]), reads=[('ps', 5)], writes=[('Vw', gt)])
                    self.normrope(ps[:, 4, :], [('ps', 4)], 8, 64, None, self.ropeA[ri][:, 0:64], self.ropeB[ri][:, 0:64], ('rope', ri),
                                  self.qr[:, 0:512], 'qr', False)
                    S.dma('sp', self.Qd[gt * 128:(gt + 1) * 128, :], self.qr[:, 0:512], reads=['qr'], writes=[('Qd', gt)])
                    self.normrope(ps[:, 5, 0:128], [('ps', 5)], 2, 64, None, self.ropeA[ri][:, 0:64], self.ropeB[ri][:, 0:64], ('rope', ri),
                                  kr[:, :], 'kr', False)
                    tb_ = 6 + self.rtp.next()
                    pT = ps[:, tb_, :].bitcast(BF16)
                    for h in range(2):
                        S.op('pe', lambda: nc.tensor.transpose(out=pT[0:64, h * 128:(h + 1) * 128], in_=kr[:, h * 64:(h + 1) * 64],
                                                               identity=self.ident[:]), reads=['kr', 'ident'], writes=[('ps', tb_)])
                    S.op('act', lambda: nc.scalar.copy(out=KTw[:, :, gt * 128:(gt + 1) * 128],
                                                       in_=pT[0:64, 0:256].rearrange("p (h t) -> p h t", h=2)),
                         reads=[('ps', tb_)], writes=[('KTw', gt)])

            S.barrier()
            tblocks = [(0, NC_)] + [(t, 4) for t in range(NC_, NT, 4)]
            with ExitStack() as p1:
                xh = [self.sb(f"xh{k}", [128, 515], F32, p1) for k in range(2)]
                xc = [self.sb(f"xc{k}", [128, 512], F32, p1) for k in range(2)]
                xcb = [self.sb(f"xcb{k}", [128, 512], BF16, p1) for k in range(2)]
                rgs = [self.sb(f"rg{k}", [128, 512], F32, p1) for k in range(2)]
                igs = [self.sb(f"ig{k}", [128, 512], F32, p1) for k in range(2)]
                av = [self.sb(f"av{k}", [128, 512], F32, p1) for k in range(2)]
                bv = [self.sb(f"bv{k}", [128, 512], F32, p1) for k in range(2)]
                a2s = [self.sb(f"a2{k}", [128, 512], F32, p1) for k in range(2)]
                rrg = Rot(2)
                hf = [self.sb(f"hf{k}", [128, 512], F32, p1) for k in range(2)]
                r2 = Rot(2)
                rab = Rot(2)
                rhf = Rot(2)
                rgp = Rot(2)
                for (t0, nt) in tblocks:
                    n = nt * 128
                    n0 = t0 * 128
                    c0 = col_of(n0)
                    for c in range(4):
                        k = r2.next()
                        S.dma('sp', xh[k][:, 0:n + 3], self.XAd[c * 128:(c + 1) * 128, c0 - 2:c0 + n + 1],
                              reads=[('XAd', t) for t in range(max(0, t0 - 1), min(NT, t0 + nt + 1))] + [('XApad', c)], writes=[('xh', k)])
                        S.op('dve', lambda: nc.vector.tensor_scalar(out=xc[k][:, 0:n], in0=xh[k][:, 0:n], scalar1=evec[:, c, 0:1],
                                                                     scalar2=evec[:, c, 4:5], op0=ALU.mult, op1=ALU.add),
                             reads=[('xh', k), 'evec'], writes=[('xc', k)])
                        for j in range(1, 4):
                            S.op('dve', lambda: nc.vector.scalar_tensor_tensor(out=xc[k][:, 0:n], in0=xh[k][:, j:j + n], scalar=evec[:, c, j:j + 1],
                                                                               in1=xc[k][:, 0:n], op0=ALU.mult, op1=ALU.add),
                                 reads=[('xh', k), 'evec', ('xc', k)], writes=[('xc', k)])
                        S.op('act', lambda: nc.scalar.copy(out=xcb[k][:, 0:n], in_=xc[k][:, 0:n]), reads=[('xc', k)], writes=[('xcb', k)])
                        for d in range(2):
                            gp = 2 * rgp.next()
                            gi_ = rrg.next()
                            rg, ig, a2 = rgs[gi_], igs[gi_], a2s[gi_]
                            S.op('pe', lambda: nc.tensor.matmul(ps[:, gp, 0:n], lhsT=BD[:, (d * 2 + 0) * 4 + c, :], rhs=xcb[k][:, 0:n],
                                                                start=True, stop=True), reads=['BD', ('xcb', k)], writes=[('ps', gp)])
                            S.op('pe', lambda: nc.tensor.matmul(ps[:, gp + 1, 0:n], lhsT=BD[:, (d * 2 + 1) * 4 + c, :], rhs=xcb[k][:, 0:n],
                                                                start=True, stop=True), reads=['BD', ('xcb', k)], writes=[('ps', gp + 1)])
                            S.op('act', lambda: nc.scalar.activation(out=rg[:, 0:n], in_=ps[:, gp, 0:n], func=AF.Sigmoid, bias=evec[:, c, 5 + d:6 + d]),
                                 reads=[('ps', gp), 'evec'], writes=[('rg', gi_)])
                            S.op('act', lambda: nc.scalar.activation(out=ig[:, 0:n], in_=ps[:, gp + 1, 0:n], func=AF.Sigmoid, bias=evec[:, c, 7 + d:8 + d]),
                                 reads=[('ps', gp + 1), 'evec'], writes=[('ig', gi_)])
                            ai = rab.next()
                            S.op('act', lambda: nc.scalar.activation(out=av[ai][:, 0:n], in_=rg[:, 0:n], func=AF.Exp, scale=csT[:, c, d:d + 1]),
                                 reads=[('rg', gi_), 'csT'], writes=[('av', ai)])
                            S.op('pool', lambda: nc.gpsimd.tensor_tensor(out=a2[:, 0:n], in0=av[ai][:, 0:n], in1=av[ai][:, 0:n], op=ALU.mult),
                                 reads=[('av', ai)], writes=[('a2', gi_)])
                            S.op('act', lambda: nc.scalar.activation(out=a2[:, 0:n], in_=a2[:, 0:n], func=AF.Sqrt, scale=-1.0, bias=self.one_t[:]),
                                 reads=[('a2', gi_), 'one_t'], writes=[('a2', gi_)])
                            S.op('pool', lambda: nc.gpsimd.tensor_tensor(out=bv[ai][:, 0:n], in0=a2[:, 0:n], in1=ig[:, 0:n], op=ALU.mult),
                                 reads=[('a2', gi_), ('ig', gi_)], writes=[('bv', ai)])
                            S.op('pool', lambda: nc.gpsimd.tensor_tensor(out=bv[ai][:, 0:n], in0=bv[ai][:, 0:n], in1=xc[k][:, 0:n], op=ALU.mult),
                                 reads=[('bv', ai), ('xc', k)], writes=[('bv', ai)])
                            if d == 0:
                                hi = rhf.next()
                                S.op('dve', lambda: nc.vector.tensor_tensor_scan(out=hf[hi][:, 0:n], data0=av[ai][:, 0:n], data1=bv[ai][:, 0:n],
                                                                                 initial=stf[:, c:c + 1], op0=ALU.mult, op1=ALU.add),
                                     reads=[('av', ai), ('bv', ai), ('stf', c)], writes=[('hf', hi)])
                                S.op('dve', lambda: nc.vector.tensor_copy(out=stf[:, c:c + 1], in_=hf[hi][:, n - 1:n]), reads=[('hf', hi)], writes=[('stf', c)])
                                S.dma('sp', self.HFd[c * 128:(c + 1) * 128, n0:n0 + n], hf[hi][:, 0:n], reads=[('hf', hi)], writes=[('HFd', t0, c)])
                            else:
                                S.dma('sp', self.ABd[c * 128:(c + 1) * 128, n0:n0 + n], av[ai][:, 0:n], reads=[('av', ai)], writes=[('ABd', t0, c)])
                                S.dma('sp', self.BBd[c * 128:(c + 1) * 128, n0:n0 + n], bv[ai][:, 0:n], reads=[('bv', ai)], writes=[('BBd', t0, c)])

            S.barrier()
            with ExitStack() as p2:
                ab = [self.sb(f"ab{k}", [128, 512], F32, p2) for k in range(1)]
                bb = [self.sb(f"bb{k}", [128, 512], F32, p2) for k in range(1)]
                hfl = [self.sb(f"hfl{k}", [128, 512], F32, p2) for k in range(1)]
                ggl = [self.sb(f"ggl{k}", [128, 512], F32, p2) for k in range(1)]
                hb = [self.sb(f"hb{k}", [128, 512], F32, p2) for k in range(1)]
                yaT = self.sb("yaT", [128, 4, 512], BF16, p2)
                obT = self.sb("obT", [64, 8, 512], BF16, p2)
                ql = [self.sb(f"ql{k}", [128, 512], BF16, p2) for k in range(2)]
                QTw = self.sb("QTw", [64, 8, 128], BF16, p2)
                pw = [self.sb(f"pw{k}", [128, 512], BF16, p2) for k in range(2)]
                rdw = self.sb("rdw", [64, 512], F32, p2)
                r2 = Rot(1)
                rq = Rot(2)
                rpw = Rot(2)
                rS = Rot(2)
                ry = Rot(2)
                order = [tblocks[0]] + list(reversed(tblocks[1:]))
                nlt = NT - NC_
                for (t0, nt) in order:
                    n = nt * 128
                    n0 = t0 * 128
                    for c in range(4):
                        k = r2.next()
                        rows = slice(c * 128, (c + 1) * 128)
                        S.dma('sp', ab[k][:, 0:n], self.ABd[rows, n0:n0 + n], reads=[('ABd', t0, c)], writes=[('ab', k)])
                        S.dma('sp', bb[k][:, 0:n], self.BBd[rows, n0:n0 + n], reads=[('BBd', t0, c)], writes=[('bb', k)])
                        S.dma('sp', hfl[k][:, 0:n], self.HFd[rows, n0:n0 + n], reads=[('HFd', t0, c)], writes=[('hfl', k)])
                        S.dma('sp', ggl[k][:, 0:n], self.GGd[rows, n0:n0 + n], reads=[('GGd', t) for t in range(t0, t0 + nt)], writes=[('ggl', k)])
                        S.op('dve', lambda: nc.vector.tensor_tensor_scan(out=hb[k][:, 0:n][:, ::-1],
                                                                         data0=ab[k][:, 0:n][:, ::-1], data1=bb[k][:, 0:n][:, ::-1],
                                                                         initial=stb[:, c:c + 1], op0=ALU.mult, op1=ALU.add),
                             reads=[('ab', k), ('bb', k), ('stb', c)], writes=[('hb', k)])
                        S.op('dve', lambda: nc.vector.tensor_copy(out=stb[:, c:c + 1], in_=hb[k][:, 0:1]), reads=[('hb', k)], writes=[('stb', c)])
                        S.op('pool', lambda: nc.gpsimd.tensor_tensor(out=hb[k][:, 0:n], in0=hb[k][:, 0:n], in1=hfl[k][:, 0:n], op=ALU.add),
                             reads=[('hb', k), ('hfl', k)], writes=[('hb', k)])
                        S.op('pool', lambda: nc.gpsimd.tensor_tensor(out=yaT[:, c, 0:n], in0=hb[k][:, 0:n], in1=ggl[k][:, 0:n], op=ALU.mult),
                             reads=[('hb', k), ('ggl', k)], writes=[('yaT', c)])
                    for tl in range(nt):
                        gt = t0 + tl
                        is_ctx = gt < NC_
                        qi = rq.next()
                        S.dma('sp', ql[qi][:], self.Qd[gt * 128:(gt + 1) * 128, :], reads=[('Qd', gt)], writes=[('ql', qi)])
                        tb_ = 6 + self.rtp.next()
                        pT = ps[:, tb_, :].bitcast(BF16)
                        for h in range(8):
                            S.op('pe', lambda: nc.tensor.transpose(out=pT[0:64, h * 128:(h + 1) * 128], in_=ql[qi][:, h * 64:(h + 1) * 64],
                                                                   identity=self.ident[:]), reads=[('ql', qi), 'ident'], writes=[('ps', tb_)])
                        S.op('act', lambda: nc.scalar.copy(out=QTw[:].rearrange("p h t -> p (h t)"), in_=pT[0:64, :]), reads=[('ps', tb_)], writes=['QTw'])
                        if is_ctx:
                            chunks = [(0, None), (1, None)]
                        else:
                            lt = gt - NC_
                            chunks = []
                            if lt > 0:
                                chunks.append((gt - 1, mlo))
                            chunks.append((gt, None))
                            if lt < nlt - 1:
                                chunks.append((gt + 1, mhi))
                            chunks += [(0, None), (1, None)]
                        for kv in range(2):
                            rhs_q = QTw[:, 4 * kv:4 * kv + 4, :].rearrange("p h t -> p (h t)")
                            for ci, (c, mask) in enumerate(chunks):
                                sbk = rS.next()
                                S.op('pe', lambda: nc.tensor.matmul(ps[:, sbk, :], lhsT=KTw[:, kv, c * 128:(c + 1) * 128], rhs=rhs_q,
                                                                    start=True, stop=(mask is None)), reads=[('KTw', c), 'QTw'], writes=[('ps', sbk)])
                                if mask is not None:
                                    S.op('pe', lambda: nc.tensor.matmul(ps[:, sbk, :], lhsT=self.ident[:], rhs=mask[:], start=False, stop=True),
                                         reads=['ident', 'masks'], writes=[('ps', sbk)])
                                pi = rpw.next()
                                S.op('act', lambda: nc.scalar.activation(out=pw[pi][:], in_=ps[:, sbk, :], func=AF.Exp, scale=0.125),
                                     reads=[('ps', sbk)], writes=[('pw', pi)])
                                S.op('pe', lambda: nc.tensor.matmul(ps[0:64, 2, :], lhsT=Vw[:, c, kv * 64:(kv + 1) * 64], rhs=pw[pi][:],
                                                                    start=(ci == 0), stop=(ci == len(chunks) - 1)),
                                     reads=[('Vw', c), ('pw', pi)], writes=[('ps', 2)])
                                S.op('pe', lambda: nc.tensor.matmul(ps[0:64, 3, :], lhsT=self.ones_bf[:, 0:64], rhs=pw[pi][:],
                                                                    start=(ci == 0), stop=False), reads=['ones_bf', ('pw', pi)], writes=[('ps', 3)])
                            S.op('pe', lambda: nc.tensor.matmul(ps[0:64, 3, :], lhsT=self.ones_bf[0:1, 0:64], rhs=eskb[0:1, kv * 512:(kv + 1) * 512],
                                                                start=False, stop=True), reads=['ones_bf', 'eskb'], writes=[('ps', 3)])
                            S.op('dve', lambda: nc.vector.reciprocal(out=rdw[:], in_=ps[0:64, 3, :]), reads=[('ps', 3)], writes=['rdw'])
                            S.op('dve', lambda: nc.vector.tensor_tensor(out=obT[:, 4 * kv:4 * kv + 4, tl * 128:(tl + 1) * 128],
                                                                        in0=ps[0:64, 2, :].rearrange("p (h t) -> p h t", h=4),
                                                                        in1=rdw[:].rearrange("p (h t) -> p h t", h=4), op=ALU.mult),
                                 reads=[('ps', 2), 'rdw'], writes=[('obT', tl)])
                    for tl in range(nt):
                        yb = 4 + 0 * ry.next()
                        for dh in range(2):
                            for c in range(4):
                                S.op('pe', lambda: nc.tensor.matmul(ps[:, yb + dh, :], lhsT=yaT[:, c, tl * 128:(tl + 1) * 128],
                                                                    rhs=woutA[:, c, dh * 512:(dh + 1) * 512], start=(c == 0), stop=False),
                                     reads=[('yaT', c), 'wout'], writes=[('ps', yb + dh)])
                            for h in range(8):
                                S.op('pe', lambda: nc.tensor.matmul(ps[:, yb + dh, :], lhsT=obT[:, h, tl * 128:(tl + 1) * 128],
                                                                    rhs=woutB[:, h, dh * 512:(dh + 1) * 512], start=False, stop=(h == 7)),
                                     reads=[('obT', tl), 'wout'], writes=[('ps', yb + dh)])
                        self.epilogue(t0 + tl, yb)

def _fm(v):
    v = np.asarray(v, np.float32)
    lead = v.shape[:-1]
    return np.ascontiguousarray(np.moveaxis(v.reshape(lead + (8, 128)), -1, 0))


def rope_tables(seq, hd):
    T = seq
    pos = np.arange(T)
    row = (pos // 64).astype(np.float32)
    col = (pos % 64).astype(np.float32)
    nf = hd // 4
    freq = (np.float32(10000.0) ** (-np.arange(nf, dtype=np.float32) / np.float32(nf))).astype(np.float32)
    ang = np.concatenate([row[:, None] * freq, col[:, None] * freq], axis=-1).astype(np.float32)
    cos, sin = np.cos(ang).astype(np.float32), np.sin(ang).astype(np.float32)
    A = np.repeat(cos, 2, axis=1)
    B = np.stack([-sin, sin], axis=-1).reshape(T, hd)
    A = np.concatenate([np.ones((CTX, hd), np.float32), A], axis=0)
    B = np.concatenate([np.zeros((CTX, hd), np.float32), B], axis=0)
    return np.ascontiguousarray(A, np.float32), np.ascontiguousarray(B, np.float32)


def make_in_maps(inp, seq=SEQ):
    maps = []
    rAg, rBg = rope_tables(seq, 128)
    rAw, rBw = rope_tables(seq, 64)
    bdw = np.zeros((2, 16, 128, 128), np.float32)
    for i in range(2):
        for d in range(2):
            for gi, key in enumerate(('lru_w_a', 'lru_w_x')):
                for c in range(4):
                    for hb in range(2):
                        bdw[i, (d * 2 + gi) * 4 + c, hb * 64:(hb + 1) * 64, hb * 64:(hb + 1) * 64] = inp[key][i, d, 2 * c + hb]
    fm4 = lambda v: np.moveaxis(np.asarray(v, np.float32).reshape(v.shape[:-1] + (4, 128)), -1, 0)
    evec = np.zeros((2, 128, 4, 11), np.float32)
    for i in range(2):
        evec[i, :, :, 0:4] = np.moveaxis(fm4(inp['even_conv_w'][i]), 1, 2)
        evec[i, :, :, 4] = fm4(inp['even_conv_b'][i])
        evec[i, :, :, 5:7] = np.moveaxis(fm4(inp['lru_b_a'][i]), 1, 2)
        evec[i, :, :, 7:9] = np.moveaxis(fm4(inp['lru_b_x'][i]), 1, 2)
        evec[i, :, :, 9:11] = np.moveaxis(fm4(inp['lru_lambda'][i]), 1, 2)
    sinkrep = np.ascontiguousarray(np.repeat(np.asarray(inp['attn_sink'], np.float32), 128, axis=1))
    ii = np.arange(128)[:, None]; jj = np.arange(128)[None, :]
    NEG = np.float32(-30000.0)
    mlo = np.where(ii >= jj, np.float32(0), NEG).astype(np.float32)
    mhi = np.where(ii <= jj, np.float32(0), NEG).astype(np.float32)
    masks = np.ascontiguousarray(np.stack([np.tile(mlo, (1, 4)), np.tile(mhi, (1, 4))], axis=0))
    w_ada = np.ascontiguousarray(inp['w_ada'], np.float32)
    b_ada = np.ascontiguousarray(inp['b_ada'], np.float32)
    b_adaT = np.ascontiguousarray(np.moveaxis(b_ada.reshape(DEPTH, 72, 128), -1, 0))
    npreT = _fm(inp['norm_pre'])
    npost = np.ascontiguousarray(inp['norm_post'], np.float32)
    for b in range(NB):
        cc = np.stack([_fm(inp['c'][b]), _fm(inp['c_ctx'])], axis=-1)
        maps.append(dict(
            x=np.ascontiguousarray(inp['x'][b, :seq], np.float32),
            ctx=np.ascontiguousarray(inp['ctx'][b], np.float32),
            ccT=np.ascontiguousarray(cc, np.float32),
            w_ada=w_ada, b_adaT=b_adaT, b_ada=b_ada, npreT=npreT, npost=npost,
            wg=np.ascontiguousarray(inp['ffn_w_gate'], np.float32),
            wu=np.ascontiguousarray(inp['ffn_w_up'], np.float32),
            wd=np.ascontiguousarray(inp['ffn_w_down'], np.float32),
            owin=np.ascontiguousarray(inp['odd_w_in'], np.float32), owout=np.ascontiguousarray(inp['odd_w_out'], np.float32),
            oqn=np.ascontiguousarray(inp['odd_q_norm'], np.float32), okn=np.ascontiguousarray(inp['odd_k_norm'], np.float32),
            ropeAg=rAg, ropeBg=rBg, ropeAw=rAw, ropeBw=rBw,
            ewin=np.ascontiguousarray(inp['even_w_in'], np.float32), ewout=np.ascontiguousarray(inp['even_w_out'], np.float32),
            bdw=bdw, evec=evec, sinkrep=sinkrep, masks=masks,
        ))
    return maps


def run(inp, seq=SEQ, stop_after=None, trace=False, force_odd=False):
    kb = K(seq=seq, stop_after=stop_after, force_odd=force_odd)
    nc = kb.build()
    maps = make_in_maps(inp, seq)
    res = run_bass_kernel_spmd(nc, maps, core_ids=list(range(NB)), trace=trace)
    out = np.stack([np.asarray(r["out"]) for r in res.results], axis=0)
    return out, res, kb


def kernel(**inputs):
    out, _, _ = run(inputs)
    return out.astype(np.float32)
```
